# Optimizing a Trainium2 kernel written in Bass

```python
import jax, jax.numpy as jnp
from jax import lax
import numpy as np

D_MODEL = 4096
BATCH = 4
SEQ = 4096
DEPTH = 1
DEC_BATCH = 4
DEC_SEQ = 2048
PAST_LEN = 128

GRID_W = 64
D_MIX = D_MODEL
D_ATTN = D_MIX // 2
D_CONV = D_MIX - D_ATTN
HEAD_DIM = 128
N_ATTN_HEADS = D_ATTN // HEAD_DIM
WIN_ROWS_MAX = 8
WIN_COLS = 16
CONV_WIDTH = 31
D_FF = -(-8 * D_MODEL // (3 * 256)) * 256
D_IN = 3 * D_ATTN + 2 * D_CONV
N_MOD = 6
EPS = 1e-6
LN_EPS = 1e-5

kernel_name = "hymba_natten_conformer_adaln_encoder"


def _rmsnorm(x, g):
    x32 = x.astype(jnp.float32)
    y = x32 * lax.rsqrt(jnp.mean(x32 * x32, axis=-1, keepdims=True) + EPS)
    return (y * g.astype(jnp.float32)).astype(x.dtype)


def _neighbourhood_attention(q, k, v, rpb):
    B, S, _ = q.shape
    rows = S // GRID_W
    kr = min(WIN_ROWS_MAX, rows)
    shp = (B, rows, GRID_W, N_ATTN_HEADS, HEAD_DIM)
    qg, kg, vg = q.reshape(shp), k.reshape(shp), v.reshape(shp)
    cols = jnp.arange(GRID_W)
    col_start = jnp.clip(cols - WIN_COLS // 2, 0, GRID_W - WIN_COLS)
    col_idx = col_start[:, None] + jnp.arange(WIN_COLS)[None, :]
    col_off = col_idx - cols[:, None] + (WIN_COLS - 1)
    scale = HEAD_DIM ** -0.5

    def row_block(r):
        rs = jnp.clip(r - kr // 2, 0, rows - kr)
        q_r = lax.dynamic_index_in_dim(qg, r, axis=1, keepdims=False)
        k_band = lax.dynamic_slice_in_dim(kg, rs, kr, axis=1)
        v_band = lax.dynamic_slice_in_dim(vg, rs, kr, axis=1)
        k_win = k_band[:, :, col_idx]
        v_win = v_band[:, :, col_idx]
        row_off = rs + jnp.arange(kr) - r + (WIN_ROWS_MAX - 1)
        bias = rpb[:, row_off[:, None, None], col_off[None, :, :]]
        s = (jnp.einsum('bqhd,biqjhd->bhiqj', q_r, k_win).astype(jnp.float32) * scale
             + bias.astype(jnp.float32)[None])
        p = jax.nn.softmax(s, axis=(2, 4)).astype(v.dtype)
        return jnp.einsum('bhiqj,biqjhd->bqhd', p, v_win)

    out = lax.map(row_block, jnp.arange(rows))
    return jnp.moveaxis(out, 0, 1).reshape(B, S, D_ATTN)


def _conformer_conv(a, g, w_dw, b_dw, ln_g, ln_b):
    u = a * jax.nn.sigmoid(g)
    pad = CONV_WIDTH // 2
    u = lax.conv_general_dilated(
        u, w_dw.astype(u.dtype)[:, None, :], window_strides=(1,), padding=[(pad, pad)],
        dimension_numbers=('NWC', 'WIO', 'NWC'), feature_group_count=D_CONV) + b_dw.astype(u.dtype)
    u32 = u.astype(jnp.float32)
    mu = jnp.mean(u32, axis=-1, keepdims=True)
    var = jnp.mean(jnp.square(u32 - mu), axis=-1, keepdims=True)
    un = (u32 - mu) * lax.rsqrt(var + LN_EPS) * ln_g.astype(jnp.float32) + ln_b.astype(jnp.float32)
    return jax.nn.silu(un).astype(a.dtype)


def _encoder(x, c, w_ada, b_ada, g_mix, w_in, rpb, w_dw, b_dw, ln_g, ln_b, w_o,
             g_ffn, w_gate, w_up, w_down, g_final):
    B = x.shape[0]
    for l in range(DEPTH):
        mod = (jax.nn.silu(c) @ w_ada[l] + b_ada[l]).reshape(B, N_MOD, D_MODEL)
        sh1, sc1, gt1, sh2, sc2, gt2 = [mod[:, i, None, :] for i in range(N_MOD)]
        h = _rmsnorm(x, g_mix[l]) * (1 + sc1) + sh1
        proj = h @ w_in[l]
        q, k, v, a, gg = jnp.split(
            proj, [D_ATTN, 2 * D_ATTN, 3 * D_ATTN, 3 * D_ATTN + D_CONV], axis=-1)
        y_attn = _neighbourhood_attention(q, k, v, rpb[l])
        y_conv = _conformer_conv(a, gg, w_dw[l], b_dw[l], ln_g[l], ln_b[l])
        y = jnp.concatenate([y_attn, y_conv], axis=-1) @ w_o[l]
        x = x + gt1 * y
        h = _rmsnorm(x, g_ffn[l]) * (1 + sc2) + sh2
        f = (jax.nn.silu(h @ w_gate[l]) * (h @ w_up[l])) @ w_down[l]
        x = x + gt2 * f
    return _rmsnorm(x, g_final)


def setup_inputs(seed: int = 0) -> dict:
    key = jax.random.key(seed)
    ks = jax.random.split(key, 20)
    f32 = jnp.float32
    nrm = lambda k, shp, s: jax.random.normal(k, shp, f32) * s
    return {
        "x_prompt": nrm(ks[0], (BATCH, SEQ, D_MODEL), 1.0),
        "x_sample": nrm(ks[1], (DEC_BATCH, DEC_SEQ, D_MODEL), 1.0),
        "c_prompt": nrm(ks[2], (BATCH, D_MODEL), 1.0),
        "c_sample": nrm(ks[3], (DEC_BATCH, D_MODEL), 1.0),
        "w_ada": nrm(ks[4], (DEPTH, D_MODEL, N_MOD * D_MODEL), 0.5 * D_MODEL ** -0.5),
        "b_ada": nrm(ks[5], (DEPTH, N_MOD * D_MODEL), 0.02),
        "g_mix": 1.0 + nrm(ks[6], (DEPTH, D_MODEL), 0.02),
        "w_in": nrm(ks[7], (DEPTH, D_MODEL, D_IN), D_MODEL ** -0.5),
        "rpb": nrm(ks[8], (DEPTH, N_ATTN_HEADS, 2 * WIN_ROWS_MAX - 1, 2 * WIN_COLS - 1), 0.1),
        "w_dw": nrm(ks[9], (DEPTH, CONV_WIDTH, D_CONV), CONV_WIDTH ** -0.5),
        "b_dw": nrm(ks[10], (DEPTH, D_CONV), 0.02),
        "ln_g": 1.0 + nrm(ks[11], (DEPTH, D_CONV), 0.02),
        "ln_b": nrm(ks[12], (DEPTH, D_CONV), 0.02),
        "w_o": nrm(ks[13], (DEPTH, D_MIX, D_MODEL), D_MIX ** -0.5),
        "g_ffn": 1.0 + nrm(ks[14], (DEPTH, D_MODEL), 0.02),
        "w_gate": nrm(ks[15], (DEPTH, D_MODEL, D_FF), D_MODEL ** -0.5),
        "w_up": nrm(ks[16], (DEPTH, D_MODEL, D_FF), D_MODEL ** -0.5),
        "w_down": nrm(ks[17], (DEPTH, D_FF, D_MODEL), D_FF ** -0.5),
        "g_final": 1.0 + nrm(ks[18], (D_MODEL,), 0.02),
    }


def reference(x_prompt, x_sample, c_prompt, c_sample, w_ada, b_ada, g_mix, w_in, rpb,
              w_dw, b_dw, ln_g, ln_b, w_o, g_ffn, w_gate, w_up, w_down, g_final):
    y_prompt = _encoder(x_prompt, c_prompt, w_ada, b_ada, g_mix, w_in, rpb, w_dw, b_dw,
                        ln_g, ln_b, w_o, g_ffn, w_gate, w_up, w_down, g_final)
    y_sample = _encoder(x_sample, c_sample, w_ada, b_ada, g_mix, w_in, rpb, w_dw, b_dw,
                        ln_g, ln_b, w_o, g_ffn, w_gate, w_up, w_down, g_final)
    return (y_prompt, y_sample)
```

```python
import contextlib
import numpy as np
import concourse.bass as bass
import concourse.mybir as mybir
from concourse.bass_utils import run_bass_kernel_spmd

F32 = mybir.dt.float32
BF16 = mybir.dt.bfloat16
ALU = mybir.AluOpType
AF = mybir.ActivationFunctionType

D = 4096
KC = 32
DFF = 11008
FC = 86
NSLOT = 6
T = 512
WIN = 1024
CEN = 256
HALO = 15
NH = 16
NCV = 16
PARTS = [(0, 22), (22, 22), (44, 21), (65, 21)]
EPS = 1e-6
LN_EPS = 1e-5
NTYPE = 12
NEG = -30000.0
VARIANTS = ["F", "I", "I", "I", "I", "L"]
ENGS = ("pe", "act", "dve", "pool", "sp")


class Op:
    __slots__ = ("eng", "fn", "reads", "writes", "dma", "deps", "signal", "sem", "cnt", "idx")

    def __init__(self, eng, fn, reads, writes, dma):
        self.eng = eng
        self.fn = fn
        self.reads = reads
        self.writes = writes
        self.dma = dma
        self.deps = ()
        self.signal = dma is not None
        self.sem = None
        self.cnt = 0


class Prog:
    def __init__(self):
        self.ops = []
        self.last_w = {}
        self.readers = {}
        self.last_eng = {}
        self.last_dma = {}
        self.pending_barrier = None

    def add(self, eng, fn, reads=(), writes=(), dma=None):
        op = Op(eng, fn, tuple(reads), tuple(writes), dma)
        op.idx = len(self.ops)
        mykey = eng if dma is None else "dma:" + dma
        deps = {}
        raw = set()
        lw = self.last_w
        rd = self.readers

        def consider(p, is_raw):
            k = p.eng if p.dma is None else "dma:" + p.dma
            q = deps.get(k)
            if q is None or q.idx < p.idx:
                deps[k] = p
            if is_raw:
                raw.add(p)

        for r in op.reads:
            p = lw.get(r)
            if p is not None:
                consider(p, True)
        for w in op.writes:
            p = lw.get(w)
            if p is not None:
                consider(p, False)
            for q in rd.get(w, {}).values():
                consider(q, False)
        if self.pending_barrier is not None and eng in self.pending_barrier:
            for p in self.pending_barrier.pop(eng):
                consider(p, True)
        keep = []
        for k, p in deps.items():
            if p.eng == eng and p.dma is None and dma is None:
                if eng == "pe":
                    continue
                same_raw = [q for q in raw if q.eng == eng and q.dma is None]
                if not same_raw:
                    continue
                p = max(same_raw, key=lambda q: q.idx)
            keep.append(p)
        for p in keep:
            p.signal = True
        op.deps = tuple(keep)
        for r in op.reads:
            rd.setdefault(r, {})[mykey] = op
        for w in op.writes:
            lw[w] = op
            rd[w] = {}
        self.ops.append(op)
        if dma is None:
            self.last_eng[eng] = op
        else:
            self.last_dma[dma] = op
        return op

    def barrier(self):
        allp = list(self.last_eng.values()) + list(self.last_dma.values())
        self.pending_barrier = {e: list(allp) for e in ENGS}
        self.last_w = {}
        self.readers = {}

    def emit(self, nc, final_waits=()):
        eng_cnt = {e: 0 for e in ENGS}
        dma_cnt = {}
        for op in self.ops:
            if op.dma is not None:
                dma_cnt[op.dma] = dma_cnt.get(op.dma, 0) + 16
                op.sem = "dma:" + op.dma
                op.cnt = dma_cnt[op.dma]
            elif op.signal:
                eng_cnt[op.eng] += 1
                op.sem = "eng:" + op.eng
                op.cnt = eng_cnt[op.eng]
        sem_names = ["eng:" + e for e in ENGS if eng_cnt[e] > 0] + ["dma:" + k for k in dma_cnt]
        per_eng = {e: [o for o in self.ops if o.eng == e] for e in ENGS}
        with contextlib.ExitStack() as st:
            sems = {}
            for i, n in enumerate(sem_names):
                sems[n] = st.enter_context(nc.semaphore("s%d" % i))
            block = st.enter_context(nc.Block())

            def run(engname, eng):
                waited = {}
                for op in per_eng[engname]:
                    for p in op.deps:
                        if waited.get(p.sem, 0) < p.cnt:
                            eng.wait_ge(sems[p.sem], p.cnt)
                            waited[p.sem] = p.cnt
                    ins = op.fn(eng)
                    if op.dma is not None:
                        ins.then_inc(sems[op.sem], 16)
                    elif op.signal:
                        ins.then_inc(sems[op.sem], 1)
                if engname == "sp":
                    for k in final_waits:
                        if k in dma_cnt:
                            eng.wait_ge(sems["dma:" + k], dma_cnt[k])

            @block.tensor
            def _(e):
                run("pe", e)

            @block.scalar
            def _(e):
                run("act", e)

            @block.vector
            def _(e):
                run("dve", e)

            @block.gpsimd
            def _(e):
                run("pool", e)

            @block.sync
            def _(e):
                run("sp", e)
        return {"eng_cnt": eng_cnt, "dma_cnt": dma_cnt, "n_ops": len(self.ops)}


def attn_blocks(var):
    out = []
    for m in range(8):
        lst = []
        for p in range(4):
            special = (var == "F" and p <= 1) or (var == "L" and p >= 2)
            d = m - p
            if special:
                if 2 <= m <= 5:
                    lst.append((p, 5 + (5 - d)))
            elif 0 <= d <= 4:
                lst.append((p, 4 - d))
        out.append(lst)
    return out


def build_bias_tables(rpb):
    kpar = np.arange(2)[:, None, None, None]
    kc = np.arange(64)[None, :, None, None]
    b = np.arange(2)[None, None, :, None]
    qc = np.arange(64)[None, None, None, :]
    cs = np.clip(qc - 8, 0, 48)
    colvalid = (kc >= cs) & (kc <= cs + 15)
    coff = np.clip(kc - qc + 15, 0, 30)
    tab = np.full((NH, 2, 64, NTYPE, 2, 64), NEG, np.float32)
    for ti in range(NTYPE):
        if ti <= 4:
            d = 4 - ti
            interior = True
        else:
            d = 5 - (ti - 5)
            interior = False
        delta = 2 * d + kpar - b - 4
        ok = (delta >= -7) & (delta <= 7)
        if interior:
            ok = ok & (delta >= -4) & (delta <= 3)
        valid = np.broadcast_to(ok & colvalid, (2, 64, 2, 64))
        roff = np.broadcast_to(np.clip(delta + 7, 0, 14), (2, 64, 2, 64))
        coffb = np.broadcast_to(coff, (2, 64, 2, 64))
        vals = rpb[:, roff, coffb]
        tab[:, :, :, ti, :, :] = np.where(valid[None], vals, np.float32(NEG))
    return np.ascontiguousarray(tab.reshape(NH, 128, NTYPE * 128))


def build_nc():
    nc = bass.Bass("TRN2", target_bir_lowering=False, dynamic_dma_scratch_size=64)
    dt_in = lambda n, s: nc.dram_tensor(n, s, F32, kind="ExternalInput").ap()
    xw = dt_in("xw", [NSLOT * WIN, D])
    csT_d = dt_in("csT", [128, KC * NSLOT])
    w_ada = dt_in("w_ada", [D, 6 * D])
    bada6_d = dt_in("bada6", [128, 192 * NSLOT])
    g6_d = dt_in("g6", [128, 2 * KC * NSLOT])
    gfin_d = dt_in("gfin", [128, KC])
    w_in = dt_in("w_in", [D, 10240])
    w_o = dt_in("w_o", [D, D])
    w_gate = dt_in("w_gate", [D, DFF])
    w_up = dt_in("w_up", [D, DFF])
    w_down = dt_in("w_down", [DFF, D])
    wdw_d = dt_in("wdw", [128, NCV * 31])
    cvp_d = dt_in("cvp", [128, 3 * NCV])
    eb_d = dt_in("eb", [NH * 128, NTYPE * 128])
    y = nc.dram_tensor("y", [NSLOT * T, D], F32, kind="ExternalOutput").ap()
    win_bf = nc.dram_tensor("win_bf", [80, 128, D], BF16, kind="Internal").ap()
    wo_bf = nc.dram_tensor("wo_bf", [32, 128, D], BF16, kind="Internal").ap()
    wg_bf = nc.dram_tensor("wg_bf", [FC, 128, D], BF16, kind="Internal").ap()
    wu_bf = nc.dram_tensor("wu_bf", [FC, 128, D], BF16, kind="Internal").ap()
    wd_bf = nc.dram_tensor("wd_bf", [4 * 32, 128, 22 * 128], BF16, kind="Internal").ap()
    tb_bf = nc.dram_tensor("tb_bf", [NH, 128, NTYPE * 128], BF16, kind="Internal").ap()

    P = Prog()
    with contextlib.ExitStack() as st:
        sb = lambda name, shape, dt: st.enter_context(nc.sbuf_tensor(name, shape, dt))
        ARENA = sb("arena", [128, 44544], F32)
        R1 = ARENA[:, 0:16384]
        HT = R1.bitcast(BF16).rearrange("p (c t) -> p c t", c=KC)
        XT = R1.rearrange("p (c t) -> p c t", c=KC)
        YT = ARENA[:, 16384:24576].bitcast(BF16).rearrange("p (c t) -> p c t", c=KC)
        XS = [ARENA[:, 24576:28672], ARENA[:, 28672:32768]]
        ACTB = ARENA[:, 32768:38400].bitcast(BF16).rearrange("p (c t) -> p c t", c=22)
        JUNK = ARENA[:, 32768:34816].bitcast(BF16)
        WR = [ARENA[:, 38400 + i * 2048:38400 + (i + 1) * 2048].bitcast(BF16) for i in range(3)]
        WA = [ARENA[:, 24576 + i * 2048:24576 + (i + 1) * 2048] for i in range(3)]
        CST = [ARENA[:, i * 4096:(i + 1) * 4096] for i in range(2)]
        CSB = [ARENA[:, 8192 + i * 2048:8192 + (i + 1) * 2048].bitcast(BF16) for i in range(2)]
        EBS = [ARENA[:, 12288 + i * 1536:12288 + (i + 1) * 1536] for i in range(2)]
        EBB = [ARENA[:, 16384 + i * 768:16384 + (i + 1) * 768].bitcast(BF16) for i in range(2)]
        csT = ARENA[:, 18432:18432 + 192].rearrange("p (k s) -> p k s", s=NSLOT)
        bada6 = ARENA[:, 18688:18688 + 1152].rearrange("p (k s) -> p k s", s=NSLOT)
        g6 = ARENA[:, 19840:19840 + 384].rearrange("p (g k s) -> p g k s", g=2, s=NSLOT)

        modT = sb("modT", [128, 192, NSLOT], F32)
        gfin = sb("gfin_sb", [128, KC], F32)
        wdw = sb("wdw_sb", [128, NCV, 31], F32)
        cvp = sb("cvp_sb", [128, 3, NCV], F32)
        ident = sb("ident", [128, 128], F32)
        identb = sb("identb", [128, 128], BF16)
        onesb = sb("onesb", [128, 128], BF16)
        epsc = sb("epsc", [128, 2], F32)
        ssq = sb("ssq", [128, 2], F32)
        rstd1 = sb("rstd1", [128, 2], F32)
        fz = sb("fz", [128, 1], F32)
        QT = sb("qT", [128, T], BF16)
        KT = sb("kT", [128, WIN], BF16)
        VTS = sb("vTs", [128, WIN], BF16)
        VV = sb("vv", [128, 8, 128], BF16)
        TB = sb("tb", [128, NTYPE * 128], BF16)
        G = [sb("g%d" % i, [128, T], F32) for i in range(5)]
        B = [sb("b%d" % i, [128, T], BF16) for i in range(2)]
        PT = sb("pt", [128, 2560], BF16)
        ACC1 = sb("acc1", [128, T], F32)
        ACC2 = sb("acc2", [128, T], F32)
        SG = sb("sg", [128, 544], F32)
        U = [sb("u%d" % i, [128, 544], F32) for i in range(2)]
        PS = [st.enter_context(nc.psum_tensor("ps%d" % i, [128, 512], F32)) for i in range(8)]

        def dve(fn, r=(), w=()):
            return P.add("dve", fn, r, w)

        def act(fn, r=(), w=()):
            return P.add("act", fn, r, w)

        def pool(fn, r=(), w=()):
            return P.add("pool", fn, r, w)

        def pe(fn, r=(), w=()):
            return P.add("pe", fn, r, w)

        def dma(fn, key, r=(), w=()):
            return P.add("sp", fn, r, w, dma=key)

        pool(lambda e: e.memset(ident[:], 0.0), w=["ident"])
        pool(lambda e: e.affine_select(out=ident[:], in_=ident[:], pattern=[[-1, 128]], compare_op=ALU.not_equal,
                                       fill=1.0, base=0, channel_multiplier=1), r=["ident"], w=["ident"])
        dve(lambda e: e.tensor_copy(out=identb[:], in_=ident[:]), r=["ident"], w=["identb"])
        dve(lambda e: e.memset(onesb[:], 1.0), w=["onesb"])
        dve(lambda e: e.memset(epsc[:, 0:1], EPS), w=["epsc"])
        dve(lambda e: e.memset(epsc[:, 1:2], LN_EPS), w=["epsc"])
        dve(lambda e: e.memset(fz[:], 0.0), w=["fz"])
        dma(lambda e: e.dma_start(out=gfin[:], in_=gfin_d[:, :]), "m_gfin", w=["gfin"])
        dma(lambda e: e.dma_start(out=wdw[:].rearrange("p a b -> p (a b)"), in_=wdw_d[:, :]), "m_wdw", w=["wdw"])
        dma(lambda e: e.dma_start(out=cvp[:].rearrange("p a b -> p (a b)"), in_=cvp_d[:, :]), "m_cvp", w=["cvp"])
        dma(lambda e: e.dma_start(out=csT.rearrange("p k s -> p (k s)"), in_=csT_d[:, :]), "m_cst", w=["csT"])
        dma(lambda e: e.dma_start(out=bada6.rearrange("p k s -> p (k s)"), in_=bada6_d[:, :]), "m_bada", w=["bada6"])
        dma(lambda e: e.dma_start(out=g6.rearrange("p g k s -> p (g k s)"), in_=g6_d[:, :]), "m_g6", w=["g6"])

        act(lambda e: e.activation(out=csT, in_=csT, func=AF.Silu), r=["csT"], w=["csT"])

        def ada_step(i):
            ng, kc = divmod(i, KC)
            bank = PS[ng % 2]
            bk = "P%d" % (ng % 2)
            s = i % 3
            dma(lambda e: e.dma_start(out=WA[s], in_=w_ada[kc * 128:(kc + 1) * 128, ng * 2048:(ng + 1) * 2048]),
                "wa%d" % s, w=["WA%d" % s])
            for j in range(16):
                pe(lambda e, j=j: e.matmul(bank[:, j * 6:(j + 1) * 6], lhsT=WA[s][:, j * 128:(j + 1) * 128],
                                           rhs=csT[:, kc, :], start=(kc == 0 and j == 0), stop=(kc == KC - 1),
                                           skip_group_check=True),
                   r=["WA%d" % s, "csT"], w=[bk])
            if kc == KC - 1:
                dve(lambda e: e.tensor_tensor(out=modT[:, ng * 16:(ng + 1) * 16, :],
                                              in0=bank[:, 0:96].rearrange("p (k s) -> p k s", s=NSLOT),
                                              in1=bada6[:, ng * 16:(ng + 1) * 16, :], op=ALU.add),
                    r=[bk, "bada6"], w=["modT"])

        conv_jobs = []
        for t in range(80):
            u_, r_ = divmod(t, 5)
            cb = r_ * 16 + u_
            conv_jobs.append((w_in[:, cb * 128:(cb + 1) * 128], KC, win_bf[t]))
        for n in range(32):
            conv_jobs.append((w_o[:, n * 128:(n + 1) * 128], KC, wo_bf[n]))
        cast_engs = ["dve", "act", "pool"]

        def conv_load(i):
            src, cnt, dst = conv_jobs[i]
            s = i % 2
            dma(lambda e: e.dma_start(
                out=CST[s][:, 0:cnt * 128].rearrange("p (k n) -> p k n", n=128),
                in_=src.rearrange("(k p) n -> p k n", p=128)), "cl%d" % s, w=["CST%d" % s])

        def conv_step(i):
            if i + 1 < len(conv_jobs):
                conv_load(i + 1)
            src, cnt, dst = conv_jobs[i]
            s = i % 2
            eng = cast_engs[i % 3]
            if eng == "act":
                act(lambda e: e.activation(out=CSB[s][:, 0:cnt * 128], in_=CST[s][:, 0:cnt * 128], func=AF.Copy),
                    r=["CST%d" % s], w=["CSB%d" % s])
            else:
                P.add(eng, lambda e: e.tensor_copy(out=CSB[s][:, 0:cnt * 128], in_=CST[s][:, 0:cnt * 128]),
                      ["CST%d" % s], ["CSB%d" % s])
            dma(lambda e: e.dma_start(out=dst, in_=CSB[s][:, 0:cnt * 128]), "cs%d" % s, r=["CSB%d" % s])

        conv_load(0)
        n_ada = 12 * KC
        for i in range(max(n_ada, len(conv_jobs))):
            if i < n_ada:
                ada_step(i)
            if i < len(conv_jobs):
                conv_step(i)
        dve(lambda e: e.scalar_tensor_tensor(out=modT[:, 32:64, :], in0=modT[:, 32:64, :], scalar=1.0, in1=g6[:, 0],
                                             op0=ALU.add, op1=ALU.mult), r=["modT", "g6"], w=["modT"])
        dve(lambda e: e.scalar_tensor_tensor(out=modT[:, 128:160, :], in0=modT[:, 128:160, :], scalar=1.0, in1=g6[:, 1],
                                             op0=ALU.add, op1=ALU.mult), r=["modT", "g6"], w=["modT"])

        for h in range(NH):
            s = h % 2
            dma(lambda e, s=s, h=h: e.dma_start(out=EBS[s], in_=eb_d[h * 128:(h + 1) * 128, :]), "ebl%d" % s, w=["EBS%d" % s])
            act(lambda e, s=s: e.activation(out=EBB[s], in_=EBS[s], func=AF.Exp), r=["EBS%d" % s], w=["EBB%d" % s])
            dma(lambda e, s=s, h=h: e.dma_start(out=tb_bf[h], in_=EBB[s]), "ebs%d" % s, r=["EBB%d" % s])

        P.barrier()

        wtiles = []
        for s_ in range(NSLOT):
            lazy = (s_ == 0)
            for t in range(80):
                wtiles.append((win_bf[t], D, "S.in.%d" % t, None))
            for n in range(32):
                wtiles.append((wo_bf[n], D, "S.o.%d" % n, None))
            for pi, (k0, cnt) in enumerate(PARTS):
                for f in range(k0, k0 + cnt):
                    wtiles.append((wg_bf[f], D, "S.g.%d" % f, (w_gate[:, f * 128:(f + 1) * 128], KC) if lazy else None))
                    wtiles.append((wu_bf[f], D, "S.u.%d" % f, (w_up[:, f * 128:(f + 1) * 128], KC) if lazy else None))
                for n in range(32):
                    wtiles.append((wd_bf[pi * 32 + n][:, 0:cnt * 128], cnt * 128, "S.d.%d.%d" % (pi, n),
                                   (w_down[k0 * 128:(k0 + cnt) * 128, n * 128:(n + 1) * 128], cnt) if lazy else None))
        wstate = {"next_load": 0, "next_use": 0, "pending": [], "ncv": 0}

        def w_flush_store(keep=0):
            while len(wstate["pending"]) > keep:
                s, dst, ncols, skey = wstate["pending"].pop(0)
                dma(lambda e, s=s, dst=dst, ncols=ncols: e.dma_start(out=dst, in_=WR[s][:, 0:ncols]), "ws%d" % s, r=["W%d" % s], w=[skey])

        def w_prefetch():
            i = wstate["next_load"]
            if i >= len(wtiles):
                w_flush_store(0)
                return
            src, ncols, skey, lazy = wtiles[i]
            w_flush_store(keep=(0 if lazy is None else 1))
            s = i % 3
            if lazy is None:
                dma(lambda e: e.dma_start(out=WR[s][:, 0:ncols], in_=src), "w%d" % s, r=[skey], w=["W%d" % s])
            else:
                fsrc, cnt = lazy
                c = wstate["ncv"]
                wstate["ncv"] = c + 1
                b = c % 2
                dma(lambda e: e.dma_start(out=XS[b][:, 0:cnt * 128].rearrange("p (k n) -> p k n", n=128),
                                          in_=fsrc.rearrange("(k p) n -> p k n", p=128)), "x%d" % b, w=["XS%d" % b])
                eng = ("dve", "act", "pool")[c % 3]
                if eng == "act":
                    act(lambda e: e.activation(out=WR[s][:, 0:ncols], in_=XS[b][:, 0:ncols], func=AF.Copy), r=["XS%d" % b], w=["W%d" % s])
                else:
                    P.add(eng, lambda e: e.tensor_copy(out=WR[s][:, 0:ncols], in_=XS[b][:, 0:ncols]), ["XS%d" % b], ["W%d" % s])
                wstate["pending"].append((s, src, ncols, skey))
            wstate["next_load"] = i + 1

        def w_next():
            i = wstate["next_use"]
            wstate["next_use"] = i + 1
            s = i % 3
            return WR[s], "W%d" % s

        for _ in range(3):
            w_prefetch()

        accstate = {"i": 0}

        def acc(n):
            i = accstate["i"]
            if n == 2 and i % 2 == 1:
                i += 1
            i = i % 4
            accstate["i"] = (i + n) % 4
            return list(range(i, i + n))

        def gemm(rhs_list, rkeys, nk=KC, banks=None):
            wt, wkey = w_next()
            for kc in range(nk):
                for (bk, c0, ncol, rf) in rhs_list:
                    pe(lambda e, kc=kc, bk=bk, c0=c0, ncol=ncol, rf=rf, wt=wt: e.matmul(
                        PS[bk][:, c0:c0 + ncol], lhsT=wt[:, kc * 128:(kc + 1) * 128], rhs=rf(kc),
                        start=(kc == 0), stop=(kc == nk - 1)),
                       r=[wkey] + rkeys, w=["P%d" % bk])
            w_prefetch()

        XTK = ["XT.%d" % c for c in range(KC)]
        YTK = ["YT.%d" % c for c in range(KC)]
        H2K = ["H2.%d" % c for c in range(KC)]
        HTW = ["HT.%d" % t for t in range(8)]

        def fence(r, w):
            dve(lambda e: e.memset(fz[:], 0.0), r=r, w=list(w) + ["fz"])

        SCALE = 128 ** -0.5

        def do_slot(slot):
            var = VARIANTS[slot]
            mcol = lambda kind, c: modT[:, kind * 32 + c, slot:slot + 1]
            fence([], HTW + ["HTall"] + XTK + YTK + H2K + ["H2all"])

            for tq in range(8):
                b = tq % 2
                xs = XS[b]
                xk = "XS%d" % b
                dma(lambda e, xs=xs, tq=tq: e.dma_start(out=xs, in_=xw[slot * WIN + tq * 128: slot * WIN + (tq + 1) * 128, :]),
                    "x%d" % b, w=[xk])
                jk = ["AB.%d" % k for k in range(8)]
                act(lambda e, xs=xs, b=b: e.activation(out=JUNK, in_=xs, func=AF.Square, accum_out=ssq[:, b:b + 1]),
                    r=[xk], w=jk + ["ssq%d" % b])
                act(lambda e, b=b: e.activation(out=rstd1[:, b:b + 1], in_=ssq[:, b:b + 1], func=AF.Sqrt, bias=epsc[:, 0:1], scale=1.0 / D),
                    r=["ssq%d" % b, "epsc"], w=["rstd%d" % b])
                dve(lambda e, b=b: e.reciprocal(out=rstd1[:, b:b + 1], in_=rstd1[:, b:b + 1]), r=["rstd%d" % b], w=["rstd%d" % b])
                dve(lambda e, xs=xs, b=b: e.tensor_scalar(out=xs, in0=xs, scalar1=rstd1[:, b:b + 1], scalar2=None, op0=ALU.mult),
                    r=[xk, "rstd%d" % b], w=[xk])
                for c4 in range(8):
                    bk = 4 + (c4 % 4)
                    for q in range(4):
                        c = c4 * 4 + q
                        pe(lambda e, xs=xs, c=c, q=q, bk=bk: e.transpose(out=PS[bk][:, q * 128:(q + 1) * 128], in_=xs[:, c * 128:(c + 1) * 128], identity=ident[:]),
                           r=[xk, "ident"], w=["P%d" % bk])
                    for q in range(4):
                        c = c4 * 4 + q
                        if q % 2 == 0:
                            dve(lambda e, c=c, q=q, bk=bk, tq=tq: e.tensor_scalar(out=HT[:, c, tq * 128:(tq + 1) * 128], in0=PS[bk][:, q * 128:(q + 1) * 128],
                                                                                scalar1=mcol(1, c), scalar2=mcol(0, c), op0=ALU.mult, op1=ALU.add),
                                r=["P%d" % bk, "modT"], w=["HT.%d" % tq])
                        else:
                            act(lambda e, c=c, q=q, bk=bk, tq=tq: e.activation(out=HT[:, c, tq * 128:(tq + 1) * 128], in_=PS[bk][:, q * 128:(q + 1) * 128],
                                                                             func=AF.Identity, scale=mcol(1, c), bias=mcol(0, c)),
                                r=["P%d" % bk, "modT"], w=["HT.%d" % tq])
            fence(HTW, ["HTall"])

            blocks = attn_blocks(var)
            live = [(m, lst) for m, lst in enumerate(blocks) if lst]
            ptoff = {}
            o_ = 0
            for m, lst in live:
                ptoff[m] = o_
                o_ += len(lst) * 128
            for u in range(NH):
                dma(lambda e, u=u: e.dma_start(out=TB[:], in_=tb_bf[u]), "tb", w=["TB"])
                (bq,) = acc(1)
                gemm([(bq, 0, T, lambda kc: HT[:, kc, CEN:CEN + T])], ["HTall"])
                act(lambda e, bq=bq: e.activation(out=QT[:], in_=PS[bq][:, :], func=AF.Copy), r=["P%d" % bq], w=["QT"])
                b0, b1 = acc(2)
                gemm([(b0, 0, T, lambda kc: HT[:, kc, 0:T]), (b1, 0, T, lambda kc: HT[:, kc, T:WIN])], ["HTall"])
                act(lambda e, b0=b0: e.activation(out=KT[:, 0:T], in_=PS[b0][:, :], func=AF.Copy), r=["P%d" % b0], w=["KT"])
                dve(lambda e, b1=b1: e.tensor_copy(out=KT[:, T:WIN], in_=PS[b1][:, :]), r=["P%d" % b1], w=["KT"])
                b0, b1 = acc(2)
                gemm([(b0, 0, T, lambda kc: HT[:, kc, 0:T]), (b1, 0, T, lambda kc: HT[:, kc, T:WIN])], ["HTall"])
                act(lambda e, b0=b0: e.activation(out=VTS[:, 0:T], in_=PS[b0][:, :], func=AF.Copy), r=["P%d" % b0], w=["VTS"])
                dve(lambda e, b1=b1: e.tensor_copy(out=VTS[:, T:WIN], in_=PS[b1][:, :]), r=["P%d" % b1], w=["VTS"])
                vps = PS[b0][:].bitcast(BF16)
                for m in range(8):
                    pe(lambda e, m=m, vps=vps: e.transpose(out=vps[:, m * 128:(m + 1) * 128], in_=VTS[:, m * 128:(m + 1) * 128], identity=identb[:]),
                       r=["VTS", "identb"], w=["P%d" % b0])
                act(lambda e, vps=vps: e.activation(out=VV[:].rearrange("p a b -> p (a b)"), in_=vps[:, 0:1024], func=AF.Copy),
                    r=["P%d" % b0], w=["VV"])
                for idx, (m, lst) in enumerate(live):
                    p_lo = lst[0][0]
                    ncol = len(lst) * 128
                    q0 = p_lo * 128
                    sb_ = 4 if (idx % 2 == 0) else 7
                    po = ptoff[m]
                    pk = "PT.%d" % m
                    pe(lambda e, m=m, q0=q0, ncol=ncol, sb_=sb_: e.matmul(PS[sb_][:, 0:ncol], lhsT=KT[:, m * 128:(m + 1) * 128],
                                                                      rhs=QT[:, q0:q0 + ncol], start=True, stop=True),
                       r=["KT", "QT"], w=["P%d" % sb_])
                    act(lambda e, sb_=sb_, ncol=ncol, po=po: e.activation(out=PT[:, po:po + ncol], in_=PS[sb_][:, 0:ncol], func=AF.Exp, scale=SCALE),
                        r=["P%d" % sb_], w=[pk])
                    runs = []
                    for (p, ti) in lst:
                        if runs and runs[-1][1] + runs[-1][2] == ti:
                            runs[-1][2] += 1
                        else:
                            runs.append([p, ti, 1])
                    for (p, ti, n_) in runs:
                        c0 = po + (p - p_lo) * 128
                        pool(lambda e, c0=c0, ti=ti, n_=n_: e.tensor_tensor(
                            out=PT[:, c0:c0 + n_ * 128], in0=PT[:, c0:c0 + n_ * 128], in1=TB[:, ti * 128:(ti + n_) * 128], op=ALU.mult),
                            r=[pk, "TB"], w=[pk])
                j = u
                ub = j % 2
                uu = U[ub]
                uk = "U%d" % ub
                ca, cb_ = (G[1], G[2]) if ub == 0 else (G[3], G[4])
                cak, cbk = ("G1", "G2") if ub == 0 else ("G3", "G4")
                a0, a1 = acc(2)
                gemm([(a0, 0, T, lambda kc: HT[:, kc, CEN - HALO:CEN - HALO + T]),
                      (a1, 0, 2 * HALO, lambda kc: HT[:, kc, CEN - HALO + T:CEN + T + HALO])], ["HTall"])
                g0, g1 = acc(2)
                gemm([(g0, 0, T, lambda kc: HT[:, kc, CEN - HALO:CEN - HALO + T]),
                      (g1, 0, 2 * HALO, lambda kc: HT[:, kc, CEN - HALO + T:CEN + T + HALO])], ["HTall"])
                act(lambda e, g0=g0: e.activation(out=SG[:, 0:T], in_=PS[g0][:, :], func=AF.Sigmoid), r=["P%d" % g0], w=["SG"])
                act(lambda e, g1=g1: e.activation(out=SG[:, T:T + 2 * HALO], in_=PS[g1][:, 0:2 * HALO], func=AF.Sigmoid), r=["P%d" % g1], w=["SG"])
                dve(lambda e, uu=uu, a0=a0: e.tensor_tensor(out=uu[:, 0:T], in0=PS[a0][:, :], in1=SG[:, 0:T], op=ALU.mult),
                    r=["P%d" % a0, "SG"], w=[uk])
                dve(lambda e, uu=uu, a1=a1: e.tensor_tensor(out=uu[:, T:T + 2 * HALO], in0=PS[a1][:, 0:2 * HALO], in1=SG[:, T:T + 2 * HALO], op=ALU.mult),
                    r=["P%d" % a1, "SG"], w=[uk])
                if var == "F":
                    dve(lambda e, uu=uu: e.memset(uu[:, 0:HALO], 0.0), r=[uk], w=[uk])
                if var == "L":
                    dve(lambda e, uu=uu: e.memset(uu[:, T + HALO:T + 2 * HALO], 0.0), r=[uk], w=[uk])
                for idx, (m, lst) in enumerate(live):
                    p_lo = lst[0][0]
                    ncol = len(lst) * 128
                    q0 = p_lo * 128
                    po = ptoff[m]
                    pk = "PT.%d" % m
                    first = idx == 0
                    last = idx == len(live) - 1
                    pe(lambda e, m=m, q0=q0, ncol=ncol, po=po, first=first, last=last: e.matmul(
                        PS[5][:, q0:q0 + ncol], lhsT=VV[:, m, :], rhs=PT[:, po:po + ncol], start=first, stop=last, skip_group_check=True),
                       r=["VV", pk], w=["P5"])
                    pe(lambda e, q0=q0, ncol=ncol, po=po, first=first, last=last: e.matmul(
                        PS[6][:, q0:q0 + ncol], lhsT=onesb[:], rhs=PT[:, po:po + ncol], start=first, stop=last, skip_group_check=True),
                       r=["onesb", pk], w=["P6"])
                dve(lambda e: e.reciprocal(out=G[0][:], in_=PS[6][:, :]), r=["P6"], w=["G0"])
                dve(lambda e, u=u: e.tensor_tensor(out=YT[:, u, :], in0=PS[5][:, :], in1=G[0][:], op=ALU.mult), r=["P5", "G0"], w=["YT.%d" % u])
                dve(lambda e, uu=uu, ca=ca, j=j: e.tensor_scalar(out=ca[:], in0=uu[:, 0:T], scalar1=wdw[:, j, 0:1], scalar2=cvp[:, 0, j:j + 1],
                                                                 op0=ALU.mult, op1=ALU.add), r=[uk, "wdw", "cvp"], w=[cak])
                dve(lambda e, uu=uu, cb_=cb_, j=j: e.tensor_scalar(out=cb_[:], in0=uu[:, 1:1 + T], scalar1=wdw[:, j, 1:2], scalar2=None, op0=ALU.mult),
                    r=[uk, "wdw"], w=[cbk])
                for k in range(2, 31):
                    cc, ck = (ca, cak) if k % 2 == 0 else (cb_, cbk)
                    dve(lambda e, uu=uu, cc=cc, j=j, k=k: e.scalar_tensor_tensor(out=cc[:], in0=uu[:, k:k + T], scalar=wdw[:, j, k:k + 1], in1=cc[:],
                                                                                 op0=ALU.mult, op1=ALU.add), r=[uk, "wdw", ck], w=[ck])
                dve(lambda e, ca=ca, cb_=cb_: e.tensor_tensor(out=ca[:], in0=ca[:], in1=cb_[:], op=ALU.add), r=[cak, cbk], w=[cak])
                act(lambda e, ca=ca, j=j: e.activation(out=YT[:, 16 + j, :], in_=ca[:], func=AF.Copy), r=[cak], w=["YT.%d" % (16 + j)])
                act(lambda e, ca=ca, cb_=cb_: e.activation(out=cb_[:], in_=ca[:], func=AF.Square), r=[cak], w=[cbk])
                if j == 0:
                    pool(lambda e, ca=ca: e.tensor_copy(out=ACC1[:], in_=ca[:]), r=[cak], w=["ACC1"])
                    pool(lambda e, cb_=cb_: e.tensor_copy(out=ACC2[:], in_=cb_[:]), r=[cbk], w=["ACC2"])
                else:
                    pool(lambda e, ca=ca: e.tensor_tensor(out=ACC1[:], in0=ACC1[:], in1=ca[:], op=ALU.add), r=[cak, "ACC1"], w=["ACC1"])
                    pool(lambda e, cb_=cb_: e.tensor_tensor(out=ACC2[:], in0=ACC2[:], in1=cb_[:], op=ALU.add), r=[cbk, "ACC2"], w=["ACC2"])
            for accT, ak, bank in ((ACC1, "ACC1", 5), (ACC2, "ACC2", 6)):
                act(lambda e, accT=accT: e.activation(out=B[0][:], in_=accT[:], func=AF.Copy), r=[ak], w=["B0"])
                dve(lambda e, accT=accT: e.tensor_tensor(out=accT[:], in0=accT[:], in1=B[0][:], op=ALU.subtract), r=[ak, "B0"], w=[ak])
                act(lambda e, accT=accT: e.activation(out=B[1][:], in_=accT[:], func=AF.Copy), r=[ak], w=["B1"])
                pe(lambda e, bank=bank: e.matmul(PS[bank][:, :], lhsT=onesb[:], rhs=B[0][:], start=True, stop=False), r=["onesb", "B0"], w=["P%d" % bank])
                pe(lambda e, bank=bank: e.matmul(PS[bank][:, :], lhsT=onesb[:], rhs=B[1][:], start=False, stop=True), r=["onesb", "B1"], w=["P%d" % bank])
            dve(lambda e: e.tensor_scalar(out=G[0][:], in0=PS[5][:, :], scalar1=1.0 / 2048, scalar2=None, op0=ALU.mult), r=["P5"], w=["G0"])
            dve(lambda e: e.tensor_tensor(out=G[2][:], in0=G[0][:], in1=G[0][:], op=ALU.mult), r=["G0"], w=["G2"])
            dve(lambda e: e.scalar_tensor_tensor(out=G[1][:], in0=PS[6][:, :], scalar=1.0 / 2048, in1=G[2][:], op0=ALU.mult, op1=ALU.subtract),
                r=["P6", "G2"], w=["G1"])
            act(lambda e: e.activation(out=G[1][:], in_=G[1][:], func=AF.Sqrt, bias=epsc[:, 1:2], scale=1.0), r=["G1", "epsc"], w=["G1"])
            dve(lambda e: e.reciprocal(out=G[1][:], in_=G[1][:]), r=["G1"], w=["G1"])
            dve(lambda e: e.scalar_tensor_tensor(out=G[2][:], in0=G[0][:], scalar=-1.0, in1=G[1][:], op0=ALU.mult, op1=ALU.mult),
                r=["G0", "G1"], w=["G2"])
            for j in range(NCV):
                tt = G[3 + j % 2]
                tk = "G%d" % (3 + j % 2)
                dve(lambda e, tt=tt, j=j: e.tensor_tensor(out=tt[:], in0=YT[:, 16 + j, :], in1=G[1][:], op=ALU.mult), r=["YT.%d" % (16 + j), "G1"], w=[tk])
                dve(lambda e, tt=tt: e.tensor_tensor(out=tt[:], in0=tt[:], in1=G[2][:], op=ALU.add), r=[tk, "G2"], w=[tk])
                act(lambda e, tt=tt, j=j: e.activation(out=YT[:, 16 + j, :], in_=tt[:], func=AF.Silu, scale=cvp[:, 1, j:j + 1], bias=cvp[:, 2, j:j + 1]),
                    r=[tk, "cvp"], w=["YT.%d" % (16 + j)])

            fence([], XTK + HTW + ["HTall"])
            for tq in range(4):
                b = tq % 2
                xs = XS[b]
                xk = "XS%d" % b
                dma(lambda e, xs=xs, tq=tq: e.dma_start(out=xs, in_=xw[slot * WIN + CEN + tq * 128: slot * WIN + CEN + (tq + 1) * 128, :]),
                    "x%d" % b, w=[xk])
                for c4 in range(8):
                    bk = 4 + (c4 % 4)
                    for q in range(4):
                        c = c4 * 4 + q
                        pe(lambda e, xs=xs, c=c, q=q, bk=bk: e.transpose(out=PS[bk][:, q * 128:(q + 1) * 128], in_=xs[:, c * 128:(c + 1) * 128], identity=ident[:]),
                           r=[xk, "ident"], w=["P%d" % bk])
                    keys = ["XT.%d" % (c4 * 4 + q) for q in range(4)]
                    if c4 % 2 == 0:
                        dve(lambda e, c4=c4, bk=bk, tq=tq: e.tensor_copy(out=XT[:, c4 * 4:c4 * 4 + 4, tq * 128:(tq + 1) * 128],
                                                                        in_=PS[bk][:, :].rearrange("p (a b) -> p a b", a=4)),
                            r=["P%d" % bk], w=keys)
                    else:
                        act(lambda e, c4=c4, bk=bk, tq=tq: e.activation(out=XT[:, c4 * 4:c4 * 4 + 4, tq * 128:(tq + 1) * 128],
                                                                       in_=PS[bk][:, :].rearrange("p (a b) -> p a b", a=4), func=AF.Copy),
                            r=["P%d" % bk], w=keys)

            for n in range(32):
                (bo,) = acc(1)
                gemm([(bo, 0, T, lambda kc: YT[:, kc, :])], YTK)
                dve(lambda e, n=n, bo=bo: e.scalar_tensor_tensor(out=XT[:, n, :], in0=PS[bo][:, :], scalar=mcol(2, n), in1=XT[:, n, :],
                                                                 op0=ALU.mult, op1=ALU.add), r=["P%d" % bo, "modT", "XT.%d" % n], w=["XT.%d" % n])

            def rms_stats():
                for c in range(KC):
                    sq = B[c % 2]
                    sk = "B%d" % (c % 2)
                    act(lambda e, sq=sq, c=c: e.activation(out=sq[:], in_=XT[:, c, :], func=AF.Square), r=["XT.%d" % c], w=[sk])
                    pe(lambda e, sq=sq, c=c: e.matmul(PS[5][:, :], lhsT=onesb[:], rhs=sq[:], start=(c == 0), stop=(c == KC - 1)),
                       r=["onesb", sk], w=["P5"])
                act(lambda e: e.activation(out=G[0][:], in_=PS[5][:, :], func=AF.Sqrt, bias=epsc[:, 0:1], scale=1.0 / D), r=["P5", "epsc"], w=["G0"])
                dve(lambda e: e.reciprocal(out=G[0][:], in_=G[0][:]), r=["G0"], w=["G0"])

            rms_stats()
            fence([], H2K + YTK + ["H2all"])
            for c in range(KC):
                tt = G[3 + c % 2]
                tk = "G%d" % (3 + c % 2)
                dve(lambda e, tt=tt, c=c: e.tensor_tensor(out=tt[:], in0=XT[:, c, :], in1=G[0][:], op=ALU.mult), r=["XT.%d" % c, "G0"], w=[tk])
                act(lambda e, tt=tt, c=c: e.activation(out=YT[:, c, :], in_=tt[:], func=AF.Identity, scale=mcol(4, c), bias=mcol(3, c)),
                    r=[tk, "modT"], w=["H2.%d" % c])
            fence(H2K, ["H2all"])

            for pi, (k0, cnt) in enumerate(PARTS):
                for fl in range(cnt):
                    (bg,) = acc(1)
                    gemm([(bg, 0, T, lambda kc: YT[:, kc, :])], ["H2all"])
                    (bu,) = acc(1)
                    gemm([(bu, 0, T, lambda kc: YT[:, kc, :])], ["H2all"])
                    sgt = G[1 + fl % 2]
                    sgk = "G%d" % (1 + fl % 2)
                    act(lambda e, sgt=sgt, bg=bg: e.activation(out=sgt[:], in_=PS[bg][:, :], func=AF.Silu), r=["P%d" % bg], w=[sgk])
                    dve(lambda e, sgt=sgt, bu=bu, fl=fl: e.tensor_tensor(out=ACTB[:, fl, :], in0=PS[bu][:, :], in1=sgt[:], op=ALU.mult),
                        r=["P%d" % bu, sgk], w=["AB.%d" % fl])
                for n in range(32):
                    (bd,) = acc(1)
                    wt, wkey = w_next()
                    for kk in range(cnt):
                        pe(lambda e, kk=kk, bd=bd, wt=wt, cnt=cnt: e.matmul(PS[bd][:, :], lhsT=wt[:, kk * 128:(kk + 1) * 128], rhs=ACTB[:, kk, :],
                                                                          start=(kk == 0), stop=(kk == cnt - 1)),
                           r=[wkey, "AB.%d" % kk], w=["P%d" % bd])
                    w_prefetch()
                    dve(lambda e, n=n, bd=bd: e.scalar_tensor_tensor(out=XT[:, n, :], in0=PS[bd][:, :], scalar=mcol(5, n), in1=XT[:, n, :],
                                                                     op0=ALU.mult, op1=ALU.add), r=["P%d" % bd, "modT", "XT.%d" % n], w=["XT.%d" % n])

            rms_stats()
            for c in range(KC):
                dve(lambda e, c=c: e.scalar_tensor_tensor(out=XT[:, c, :], in0=XT[:, c, :], scalar=gfin[:, c:c + 1], in1=G[0][:],
                                                          op0=ALU.mult, op1=ALU.mult), r=["XT.%d" % c, "gfin", "G0"], w=["XT.%d" % c])
            for tq in range(4):
                b = tq % 2
                xs = XS[b]
                xk = "XS%d" % b
                for c4 in range(8):
                    bk = 4 + (c4 % 4)
                    for q in range(4):
                        c = c4 * 4 + q
                        pe(lambda e, c=c, q=q, bk=bk, tq=tq: e.transpose(out=PS[bk][:, q * 128:(q + 1) * 128], in_=XT[:, c, tq * 128:(tq + 1) * 128], identity=ident[:]),
                           r=["XT.%d" % c, "ident"], w=["P%d" % bk])
                    if c4 % 2 == 0:
                        dve(lambda e, xs=xs, c4=c4, bk=bk: e.tensor_copy(out=xs[:, c4 * 512:(c4 + 1) * 512], in_=PS[bk][:, :]), r=["P%d" % bk], w=[xk])
                    else:
                        act(lambda e, xs=xs, c4=c4, bk=bk: e.activation(out=xs[:, c4 * 512:(c4 + 1) * 512], in_=PS[bk][:, :], func=AF.Copy), r=["P%d" % bk], w=[xk])
                dma(lambda e, xs=xs, tq=tq: e.dma_start(out=y[slot * T + tq * 128: slot * T + (tq + 1) * 128, :], in_=xs), "o%d" % b, r=[xk])

        for slot_ in range(NSLOT):
            do_slot(slot_)
        w_flush_store()
        assert wstate["next_use"] == len(wtiles), (wstate, len(wtiles))
        info = P.emit(nc, final_waits=["o0", "o1"])
    return nc, info


def slot_plan():
    plan = {}
    for i in range(4):
        plan[(i, 0)] = ("P", i, 0)
        for s in range(1, 5):
            plan[(i, s)] = ("P", i, s)
        plan[(i, 5)] = ("P", i, 7)
        c = 4 + i
        plan[(c, 0)] = ("S", i, 0)
        plan[(c, 1)] = ("S", i, 1)
        plan[(c, 2)] = ("S", i, 2)
        plan[(c, 3)] = ("P", i, 5)
        plan[(c, 4)] = ("P", i, 6)
        plan[(c, 5)] = ("S", i, 3)
    return plan


def colmajor(v, k):
    return np.ascontiguousarray(np.asarray(v, np.float32).reshape(k, 128).T)


_CACHE = {}


def kernel(x_prompt, x_sample, c_prompt, c_sample, w_ada, b_ada, g_mix, w_in, rpb, w_dw, b_dw, ln_g, ln_b,
           w_o, g_ffn, w_gate, w_up, w_down, g_final):
    f = lambda a: np.asarray(a, dtype=np.float32)
    xs_ = {"P": f(x_prompt), "S": f(x_sample)}
    cs_ = {"P": f(c_prompt), "S": f(c_sample)}
    plan = slot_plan()
    if "nc" not in _CACHE:
        _CACHE["nc"] = build_nc()
    nc, info = _CACHE["nc"]

    shared = {
        "w_ada": f(w_ada)[0],
        "bada6": np.ascontiguousarray(np.repeat(colmajor(f(b_ada)[0], 192)[:, :, None], NSLOT, axis=2).reshape(128, 192 * NSLOT)),
        "g6": np.ascontiguousarray(np.repeat(np.stack([colmajor(f(g_mix)[0], KC), colmajor(f(g_ffn)[0], KC)], axis=1)[:, :, :, None],
                                             NSLOT, axis=3).reshape(128, 2 * KC * NSLOT)),
        "gfin": colmajor(f(g_final), KC),
        "w_in": f(w_in)[0], "w_o": f(w_o)[0], "w_gate": f(w_gate)[0], "w_up": f(w_up)[0], "w_down": f(w_down)[0],
        "wdw": np.ascontiguousarray(f(w_dw)[0].T.reshape(NCV, 128, 31).transpose(1, 0, 2).reshape(128, NCV * 31)),
        "cvp": np.ascontiguousarray(np.stack([colmajor(f(b_dw)[0], NCV), colmajor(f(ln_g)[0], NCV), colmajor(f(ln_b)[0], NCV)], axis=1).reshape(128, 3 * NCV)),
        "eb": build_bias_tables(f(rpb)[0]).reshape(NH * 128, NTYPE * 128),
    }
    in_maps = []
    for core in range(8):
        xwin = np.zeros((NSLOT, WIN, D), np.float32)
        cst = np.zeros((128, KC, NSLOT), np.float32)
        for s in range(NSLOT):
            grp, si, ti = plan[(core, s)]
            xseq = xs_[grp][si]
            S = xseq.shape[0]
            t0 = ti * T - CEN
            lo, hi = max(t0, 0), min(t0 + WIN, S)
            xwin[s, lo - t0:hi - t0] = xseq[lo:hi]
            cst[:, :, s] = colmajor(cs_[grp][si], KC)
        m = dict(shared)
        m["xw"] = xwin.reshape(NSLOT * WIN, D)
        m["csT"] = np.ascontiguousarray(cst.reshape(128, KC * NSLOT))
        in_maps.append(m)
    res = run_bass_kernel_spmd(nc, in_maps, core_ids=list(range(8)))
    y_prompt = np.zeros(xs_["P"].shape, np.float32)
    y_sample = np.zeros(xs_["S"].shape, np.float32)
    outs = {"P": y_prompt, "S": y_sample}
    for core in range(8):
        yc = np.asarray(res.results[core]["y"]).reshape(NSLOT, T, D)
        for s in range(NSLOT):
            grp, si, ti = plan[(core, s)]
            outs[grp][si, ti * T:(ti + 1) * T] = yc[s]
    return (y_prompt, y_sample)
```

```python
import contextlib
import numpy as np
import concourse.bass as bass
import concourse.mybir as mybir
from concourse.bass_utils import run_bass_kernel_spmd

F32 = mybir.dt.float32
BF16 = mybir.dt.bfloat16
ALU = mybir.AluOpType
AF = mybir.ActivationFunctionType

D = 4096
KC = 32
DFF = 11008
FC = 86
NSLOT = 6
T = 512
WIN = 1024
CEN = 256
HALO = 15
NH = 16
NCV = 16
PARTS = [(0, 22), (22, 22), (44, 21), (65, 21)]
EPS = 1e-6
LN_EPS = 1e-5
NTYPE = 12
NEG = -30000.0
VARIANTS = ["F", "I", "I", "I", "I", "L"]
ENGS = ("pe", "act", "dve", "pool", "sp")


class Op:
    __slots__ = ("eng", "fn", "reads", "writes", "dma", "deps", "signal", "sem", "cnt", "idx")

    def __init__(self, eng, fn, reads, writes, dma):
        self.eng = eng
        self.fn = fn
        self.reads = reads
        self.writes = writes
        self.dma = dma
        self.deps = ()
        self.signal = dma is not None
        self.sem = None
        self.cnt = 0


class Prog:
    def __init__(self):
        self.ops = []
        self.last_w = {}
        self.readers = {}
        self.last_eng = {}
        self.last_dma = {}
        self.pending_barrier = None

    def add(self, eng, fn, reads=(), writes=(), dma=None):
        op = Op(eng, fn, tuple(reads), tuple(writes), dma)
        op.idx = len(self.ops)
        mykey = eng if dma is None else "dma:" + dma
        deps = {}
        raw = set()
        lw = self.last_w
        rd = self.readers

        def consider(p, is_raw):
            k = p.eng if p.dma is None else "dma:" + p.dma
            q = deps.get(k)
            if q is None or q.idx < p.idx:
                deps[k] = p
            if is_raw:
                raw.add(p)

        for r in op.reads:
            p = lw.get(r)
            if p is not None:
                consider(p, True)
        for w in op.writes:
            p = lw.get(w)
            if p is not None:
                consider(p, False)
            for q in rd.get(w, {}).values():
                consider(q, False)
        if self.pending_barrier is not None and eng in self.pending_barrier:
            for p in self.pending_barrier.pop(eng):
                consider(p, True)
        keep = []
        for k, p in deps.items():
            if p.eng == eng and p.dma is None and dma is None:
                if eng == "pe":
                    continue
                same_raw = [q for q in raw if q.eng == eng and q.dma is None]
                if not same_raw:
                    continue
                p = max(same_raw, key=lambda q: q.idx)
            keep.append(p)
        for p in keep:
            p.signal = True
        op.deps = tuple(keep)
        for r in op.reads:
            rd.setdefault(r, {})[mykey] = op
        for w in op.writes:
            lw[w] = op
            rd[w] = {}
        self.ops.append(op)
        if dma is None:
            self.last_eng[eng] = op
        else:
            self.last_dma[dma] = op
        return op

    def barrier(self):
        allp = list(self.last_eng.values()) + list(self.last_dma.values())
        self.pending_barrier = {e: list(allp) for e in ENGS}
        self.last_w = {}
        self.readers = {}

    def emit(self, nc, final_waits=()):
        eng_cnt = {e: 0 for e in ENGS}
        dma_cnt = {}
        for op in self.ops:
            if op.dma is not None:
                dma_cnt[op.dma] = dma_cnt.get(op.dma, 0) + 16
                op.sem = "dma:" + op.dma
                op.cnt = dma_cnt[op.dma]
            elif op.signal:
                eng_cnt[op.eng] += 1
                op.sem = "eng:" + op.eng
                op.cnt = eng_cnt[op.eng]
        sem_names = ["eng:" + e for e in ENGS if eng_cnt[e] > 0] + ["dma:" + k for k in dma_cnt]
        per_eng = {e: [o for o in self.ops if o.eng == e] for e in ENGS}
        with contextlib.ExitStack() as st:
            sems = {}
            for i, n in enumerate(sem_names):
                sems[n] = st.enter_context(nc.semaphore("s%d" % i))
            block = st.enter_context(nc.Block())

            def run(engname, eng):
                waited = {}
                for op in per_eng[engname]:
                    for p in op.deps:
                        if waited.get(p.sem, 0) < p.cnt:
                            eng.wait_ge(sems[p.sem], p.cnt)
                            waited[p.sem] = p.cnt
                    ins = op.fn(eng)
                    if op.dma is not None:
                        ins.then_inc(sems[op.sem], 16)
                    elif op.signal:
                        ins.then_inc(sems[op.sem], 1)
                if engname == "sp":
                    for k in final_waits:
                        if k in dma_cnt:
                            eng.wait_ge(sems["dma:" + k], dma_cnt[k])

            @block.tensor
            def _(e):
                run("pe", e)

            @block.scalar
            def _(e):
                run("act", e)

            @block.vector
            def _(e):
                run("dve", e)

            @block.gpsimd
            def _(e):
                run("pool", e)

            @block.sync
            def _(e):
                run("sp", e)
        return {"eng_cnt": eng_cnt, "dma_cnt": dma_cnt, "n_ops": len(self.ops)}


def attn_blocks(var):
    out = []
    for m in range(8):
        lst = []
        for p in range(4):
            special = (var == "F" and p <= 1) or (var == "L" and p >= 2)
            d = m - p
            if special:
                if 2 <= m <= 5:
                    lst.append((p, 5 + (5 - d)))
            elif 0 <= d <= 4:
                lst.append((p, 4 - d))
        out.append(lst)
    return out


def build_bias_tables(rpb):
    kpar = np.arange(2)[:, None, None, None]
    kc = np.arange(64)[None, :, None, None]
    b = np.arange(2)[None, None, :, None]
    qc = np.arange(64)[None, None, None, :]
    cs = np.clip(qc - 8, 0, 48)
    colvalid = (kc >= cs) & (kc <= cs + 15)
    coff = np.clip(kc - qc + 15, 0, 30)
    tab = np.full((NH, 2, 64, NTYPE, 2, 64), NEG, np.float32)
    for ti in range(NTYPE):
        if ti <= 4:
            d = 4 - ti
            interior = True
        else:
            d = 5 - (ti - 5)
            interior = False
        delta = 2 * d + kpar - b - 4
        ok = (delta >= -7) & (delta <= 7)
        if interior:
            ok = ok & (delta >= -4) & (delta <= 3)
        valid = np.broadcast_to(ok & colvalid, (2, 64, 2, 64))
        roff = np.broadcast_to(np.clip(delta + 7, 0, 14), (2, 64, 2, 64))
        coffb = np.broadcast_to(coff, (2, 64, 2, 64))
        vals = rpb[:, roff, coffb]
        tab[:, :, :, ti, :, :] = np.where(valid[None], vals, np.float32(NEG))
    return np.ascontiguousarray(tab.reshape(NH, 128, NTYPE * 128))


def build_nc():
    nc = bass.Bass("TRN2", target_bir_lowering=False, dynamic_dma_scratch_size=64)
    dt_in = lambda n, s: nc.dram_tensor(n, s, F32, kind="ExternalInput").ap()
    xw = dt_in("xw", [NSLOT * WIN, D])
    csT_d = dt_in("csT", [128, KC * NSLOT])
    w_ada = dt_in("w_ada", [D, 6 * D])
    bada6_d = dt_in("bada6", [128, 192 * NSLOT])
    g6_d = dt_in("g6", [128, 2 * KC * NSLOT])
    gfin_d = dt_in("gfin", [128, KC])
    w_in = dt_in("w_in_t", [80 * 128, D]).rearrange("(t p) f -> t p f", p=128)
    w_o = dt_in("w_o_t", [32 * 128, D]).rearrange("(t p) f -> t p f", p=128)
    w_gate = dt_in("w_gate_t", [FC * 128, D]).rearrange("(t p) f -> t p f", p=128)
    w_up = dt_in("w_up_t", [FC * 128, D]).rearrange("(t p) f -> t p f", p=128)
    w_down = dt_in("w_down_t", [4 * 32 * 128, 22 * 128]).rearrange("(t p) f -> t p f", p=128)
    wdw_d = dt_in("wdw", [128, NCV * 31])
    cvp_d = dt_in("cvp", [128, 3 * NCV])
    eb_d = dt_in("eb", [NH * 128, NTYPE * 128])
    y = nc.dram_tensor("y", [NSLOT * T, D], F32, kind="ExternalOutput").ap()
    win_bf = nc.dram_tensor("win_bf", [80, 128, D], BF16, kind="Internal").ap()
    wo_bf = nc.dram_tensor("wo_bf", [32, 128, D], BF16, kind="Internal").ap()
    wg_bf = nc.dram_tensor("wg_bf", [FC, 128, D], BF16, kind="Internal").ap()
    wu_bf = nc.dram_tensor("wu_bf", [FC, 128, D], BF16, kind="Internal").ap()
    wd_bf = nc.dram_tensor("wd_bf", [4 * 32, 128, 22 * 128], BF16, kind="Internal").ap()
    tb_bf = nc.dram_tensor("tb_bf", [NH, 128, NTYPE * 128], BF16, kind="Internal").ap()

    P = Prog()
    with contextlib.ExitStack() as st:
        sb = lambda name, shape, dt: st.enter_context(nc.sbuf_tensor(name, shape, dt))
        ARENA = sb("arena", [128, 44544], F32)
        R1 = ARENA[:, 0:16384]
        HT = R1.bitcast(BF16).rearrange("p (c t) -> p c t", c=KC)
        XT = R1.rearrange("p (c t) -> p c t", c=KC)
        YT = ARENA[:, 16384:24576].bitcast(BF16).rearrange("p (c t) -> p c t", c=KC)
        XS = [ARENA[:, 24576:28672], ARENA[:, 28672:32768]]
        ACTB = ARENA[:, 32768:38400].bitcast(BF16).rearrange("p (c t) -> p c t", c=22)
        JUNK = ARENA[:, 32768:34816].bitcast(BF16)
        WR = [ARENA[:, 38400 + i * 2048:38400 + (i + 1) * 2048].bitcast(BF16) for i in range(3)]
        WA = [ARENA[:, 24576 + i * 2048:24576 + (i + 1) * 2048] for i in range(3)]
        CST = [ARENA[:, i * 4096:(i + 1) * 4096] for i in range(2)]
        CSB = [ARENA[:, 8192 + i * 2048:8192 + (i + 1) * 2048].bitcast(BF16) for i in range(2)]
        EBS = [ARENA[:, 12288 + i * 1536:12288 + (i + 1) * 1536] for i in range(2)]
        EBB = [ARENA[:, 16384 + i * 768:16384 + (i + 1) * 768].bitcast(BF16) for i in range(2)]
        csT = ARENA[:, 18432:18432 + 192].rearrange("p (k s) -> p k s", s=NSLOT)
        bada6 = ARENA[:, 18688:18688 + 1152].rearrange("p (k s) -> p k s", s=NSLOT)
        g6 = ARENA[:, 19840:19840 + 384].rearrange("p (g k s) -> p g k s", g=2, s=NSLOT)

        modT = sb("modT", [128, 192, NSLOT], F32)
        gfin = sb("gfin_sb", [128, KC], F32)
        wdw = sb("wdw_sb", [128, NCV, 31], F32)
        cvp = sb("cvp_sb", [128, 3, NCV], F32)
        ident = sb("ident", [128, 128], F32)
        identb = sb("identb", [128, 128], BF16)
        onesb = sb("onesb", [128, 128], BF16)
        epsc = sb("epsc", [128, 2], F32)
        ssq = sb("ssq", [128, 2], F32)
        rstd1 = sb("rstd1", [128, 2], F32)
        fz = sb("fz", [128, 1], F32)
        QT = sb("qT", [128, T], BF16)
        KT = sb("kT", [128, WIN], BF16)
        VTS = sb("vTs", [128, WIN], BF16)
        VV = sb("vv", [128, 8, 128], BF16)
        TB = sb("tb", [128, NTYPE * 128], BF16)
        G = [sb("g%d" % i, [128, T], F32) for i in range(5)]
        B = [sb("b%d" % i, [128, T], BF16) for i in range(2)]
        PT = sb("pt", [128, 2560], BF16)
        ACC1 = sb("acc1", [128, T], F32)
        ACC2 = sb("acc2", [128, T], F32)
        SG = sb("sg", [128, 544], F32)
        U = [sb("u%d" % i, [128, 544], F32) for i in range(2)]
        PS = [st.enter_context(nc.psum_tensor("ps%d" % i, [128, 512], F32)) for i in range(8)]

        def dve(fn, r=(), w=()):
            return P.add("dve", fn, r, w)

        def act(fn, r=(), w=()):
            return P.add("act", fn, r, w)

        def pool(fn, r=(), w=()):
            return P.add("pool", fn, r, w)

        def pe(fn, r=(), w=()):
            return P.add("pe", fn, r, w)

        def dma(fn, key, r=(), w=()):
            return P.add("sp", fn, r, w, dma=key)

        pool(lambda e: e.memset(ident[:], 0.0), w=["ident"])
        pool(lambda e: e.affine_select(out=ident[:], in_=ident[:], pattern=[[-1, 128]], compare_op=ALU.not_equal,
                                       fill=1.0, base=0, channel_multiplier=1), r=["ident"], w=["ident"])
        dve(lambda e: e.tensor_copy(out=identb[:], in_=ident[:]), r=["ident"], w=["identb"])
        dve(lambda e: e.memset(onesb[:], 1.0), w=["onesb"])
        dve(lambda e: e.memset(epsc[:, 0:1], EPS), w=["epsc"])
        dve(lambda e: e.memset(epsc[:, 1:2], LN_EPS), w=["epsc"])
        dve(lambda e: e.memset(fz[:], 0.0), w=["fz"])
        dma(lambda e: e.dma_start(out=gfin[:], in_=gfin_d[:, :]), "m_gfin", w=["gfin"])
        dma(lambda e: e.dma_start(out=wdw[:].rearrange("p a b -> p (a b)"), in_=wdw_d[:, :]), "m_wdw", w=["wdw"])
        dma(lambda e: e.dma_start(out=cvp[:].rearrange("p a b -> p (a b)"), in_=cvp_d[:, :]), "m_cvp", w=["cvp"])
        dma(lambda e: e.dma_start(out=csT.rearrange("p k s -> p (k s)"), in_=csT_d[:, :]), "m_cst", w=["csT"])
        dma(lambda e: e.dma_start(out=bada6.rearrange("p k s -> p (k s)"), in_=bada6_d[:, :]), "m_bada", w=["bada6"])
        dma(lambda e: e.dma_start(out=g6.rearrange("p g k s -> p (g k s)"), in_=g6_d[:, :]), "m_g6", w=["g6"])

        act(lambda e: e.activation(out=csT, in_=csT, func=AF.Silu), r=["csT"], w=["csT"])

        def ada_step(i):
            ng, kc = divmod(i, KC)
            bank = PS[ng % 2]
            bk = "P%d" % (ng % 2)
            s = i % 3
            dma(lambda e: e.dma_start(out=WA[s], in_=w_ada[kc * 128:(kc + 1) * 128, ng * 2048:(ng + 1) * 2048]),
                "wa%d" % s, w=["WA%d" % s])
            for j in range(16):
                pe(lambda e, j=j: e.matmul(bank[:, j * 6:(j + 1) * 6], lhsT=WA[s][:, j * 128:(j + 1) * 128],
                                           rhs=csT[:, kc, :], start=(kc == 0 and j == 0), stop=(kc == KC - 1),
                                           skip_group_check=True),
                   r=["WA%d" % s, "csT"], w=[bk])
            if kc == KC - 1:
                dve(lambda e: e.tensor_tensor(out=modT[:, ng * 16:(ng + 1) * 16, :],
                                              in0=bank[:, 0:96].rearrange("p (k s) -> p k s", s=NSLOT),
                                              in1=bada6[:, ng * 16:(ng + 1) * 16, :], op=ALU.add),
                    r=[bk, "bada6"], w=["modT"])

        conv_jobs = []
        for t in range(80):
            conv_jobs.append((w_in[t], KC, win_bf[t]))
        for n in range(32):
            conv_jobs.append((w_o[n], KC, wo_bf[n]))
        cast_engs = ["dve", "act", "pool"]

        def conv_load(i):
            src, cnt, dst = conv_jobs[i]
            s = i % 2
            dma(lambda e: e.dma_start(out=CST[s][:, 0:cnt * 128], in_=src[:, 0:cnt * 128]), "cl%d" % s, w=["CST%d" % s])

        def conv_step(i):
            if i + 1 < len(conv_jobs):
                conv_load(i + 1)
            src, cnt, dst = conv_jobs[i]
            s = i % 2
            eng = cast_engs[i % 3]
            if eng == "act":
                act(lambda e: e.activation(out=CSB[s][:, 0:cnt * 128], in_=CST[s][:, 0:cnt * 128], func=AF.Copy),
                    r=["CST%d" % s], w=["CSB%d" % s])
            else:
                P.add(eng, lambda e: e.tensor_copy(out=CSB[s][:, 0:cnt * 128], in_=CST[s][:, 0:cnt * 128]),
                      ["CST%d" % s], ["CSB%d" % s])
            dma(lambda e: e.dma_start(out=dst, in_=CSB[s][:, 0:cnt * 128]), "cs%d" % s, r=["CSB%d" % s])

        conv_load(0)
        n_ada = 12 * KC
        for i in range(max(n_ada, len(conv_jobs))):
            if i < n_ada:
                ada_step(i)
            if i < len(conv_jobs):
                conv_step(i)
        dve(lambda e: e.scalar_tensor_tensor(out=modT[:, 32:64, :], in0=modT[:, 32:64, :], scalar=1.0, in1=g6[:, 0],
                                             op0=ALU.add, op1=ALU.mult), r=["modT", "g6"], w=["modT"])
        dve(lambda e: e.scalar_tensor_tensor(out=modT[:, 128:160, :], in0=modT[:, 128:160, :], scalar=1.0, in1=g6[:, 1],
                                             op0=ALU.add, op1=ALU.mult), r=["modT", "g6"], w=["modT"])

        for h in range(NH):
            s = h % 2
            dma(lambda e, s=s, h=h: e.dma_start(out=EBS[s], in_=eb_d[h * 128:(h + 1) * 128, :]), "ebl%d" % s, w=["EBS%d" % s])
            act(lambda e, s=s: e.activation(out=EBB[s], in_=EBS[s], func=AF.Exp), r=["EBS%d" % s], w=["EBB%d" % s])
            dma(lambda e, s=s, h=h: e.dma_start(out=tb_bf[h], in_=EBB[s]), "ebs%d" % s, r=["EBB%d" % s])

        P.barrier()

        wtiles = []
        for s_ in range(NSLOT):
            lazy = (s_ == 0)
            for t in range(80):
                wtiles.append((win_bf[t], D, "S.in.%d" % t, None))
            for n in range(32):
                wtiles.append((wo_bf[n], D, "S.o.%d" % n, None))
            for pi, (k0, cnt) in enumerate(PARTS):
                for f in range(k0, k0 + cnt):
                    wtiles.append((wg_bf[f], D, "S.g.%d" % f, (w_gate[f], KC) if lazy else None))
                    wtiles.append((wu_bf[f], D, "S.u.%d" % f, (w_up[f], KC) if lazy else None))
                for n in range(32):
                    wtiles.append((wd_bf[pi * 32 + n][:, 0:cnt * 128], cnt * 128, "S.d.%d.%d" % (pi, n),
                                   (w_down[pi * 32 + n], cnt) if lazy else None))
        wstate = {"next_load": 0, "next_use": 0, "pending": [], "ncv": 0}

        def w_flush_store(keep=0):
            while len(wstate["pending"]) > keep:
                s, dst, ncols, skey = wstate["pending"].pop(0)
                dma(lambda e, s=s, dst=dst, ncols=ncols: e.dma_start(out=dst, in_=WR[s][:, 0:ncols]), "ws%d" % s, r=["W%d" % s], w=[skey])

        def w_prefetch():
            i = wstate["next_load"]
            if i >= len(wtiles):
                w_flush_store(0)
                return
            src, ncols, skey, lazy = wtiles[i]
            w_flush_store(keep=(0 if lazy is None else 1))
            s = i % 3
            if lazy is None:
                dma(lambda e: e.dma_start(out=WR[s][:, 0:ncols], in_=src), "w%d" % s, r=[skey], w=["W%d" % s])
            else:
                fsrc, cnt = lazy
                c = wstate["ncv"]
                wstate["ncv"] = c + 1
                b = c % 2
                dma(lambda e: e.dma_start(out=XS[b][:, 0:cnt * 128], in_=fsrc[:, 0:cnt * 128]), "x%d" % b, w=["XS%d" % b])
                eng = ("dve", "act", "pool")[c % 3]
                if eng == "act":
                    act(lambda e: e.activation(out=WR[s][:, 0:ncols], in_=XS[b][:, 0:ncols], func=AF.Copy), r=["XS%d" % b], w=["W%d" % s])
                else:
                    P.add(eng, lambda e: e.tensor_copy(out=WR[s][:, 0:ncols], in_=XS[b][:, 0:ncols]), ["XS%d" % b], ["W%d" % s])
                wstate["pending"].append((s, src, ncols, skey))
            wstate["next_load"] = i + 1

        def w_next():
            i = wstate["next_use"]
            wstate["next_use"] = i + 1
            s = i % 3
            return WR[s], "W%d" % s

        for _ in range(3):
            w_prefetch()

        accstate = {"i": 0}

        def acc(n):
            i = accstate["i"]
            if n == 2 and i % 2 == 1:
                i += 1
            i = i % 4
            accstate["i"] = (i + n) % 4
            return list(range(i, i + n))

        def gemm(rhs_list, rkeys, nk=KC, banks=None):
            wt, wkey = w_next()
            for kc in range(nk):
                for (bk, c0, ncol, rf) in rhs_list:
                    pe(lambda e, kc=kc, bk=bk, c0=c0, ncol=ncol, rf=rf, wt=wt: e.matmul(
                        PS[bk][:, c0:c0 + ncol], lhsT=wt[:, kc * 128:(kc + 1) * 128], rhs=rf(kc),
                        start=(kc == 0), stop=(kc == nk - 1)),
                       r=[wkey] + rkeys, w=["P%d" % bk])
            w_prefetch()

        XTK = ["XT.%d" % c for c in range(KC)]
        YTK = ["YT.%d" % c for c in range(KC)]
        H2K = ["H2.%d" % c for c in range(KC)]
        HTW = ["HT.%d" % t for t in range(8)]

        def fence(r, w):
            dve(lambda e: e.memset(fz[:], 0.0), r=r, w=list(w) + ["fz"])

        SCALE = 128 ** -0.5

        def do_slot(slot):
            var = VARIANTS[slot]
            mcol = lambda kind, c: modT[:, kind * 32 + c, slot:slot + 1]
            fence([], HTW + ["HTall"] + XTK + YTK + H2K + ["H2all"])

            for tq in range(8):
                b = tq % 2
                xs = XS[b]
                xk = "XS%d" % b
                dma(lambda e, xs=xs, tq=tq: e.dma_start(out=xs, in_=xw[slot * WIN + tq * 128: slot * WIN + (tq + 1) * 128, :]),
                    "x%d" % b, w=[xk])
                jk = ["AB.%d" % k for k in range(8)]
                act(lambda e, xs=xs, b=b: e.activation(out=JUNK, in_=xs, func=AF.Square, accum_out=ssq[:, b:b + 1]),
                    r=[xk], w=jk + ["ssq%d" % b])
                act(lambda e, b=b: e.activation(out=rstd1[:, b:b + 1], in_=ssq[:, b:b + 1], func=AF.Sqrt, bias=epsc[:, 0:1], scale=1.0 / D),
                    r=["ssq%d" % b, "epsc"], w=["rstd%d" % b])
                dve(lambda e, b=b: e.reciprocal(out=rstd1[:, b:b + 1], in_=rstd1[:, b:b + 1]), r=["rstd%d" % b], w=["rstd%d" % b])
                dve(lambda e, xs=xs, b=b: e.tensor_scalar(out=xs, in0=xs, scalar1=rstd1[:, b:b + 1], scalar2=None, op0=ALU.mult),
                    r=[xk, "rstd%d" % b], w=[xk])
                for c4 in range(8):
                    bk = 4 + (c4 % 4)
                    for q in range(4):
                        c = c4 * 4 + q
                        pe(lambda e, xs=xs, c=c, q=q, bk=bk: e.transpose(out=PS[bk][:, q * 128:(q + 1) * 128], in_=xs[:, c * 128:(c + 1) * 128], identity=ident[:]),
                           r=[xk, "ident"], w=["P%d" % bk])
                    for q in range(4):
                        c = c4 * 4 + q
                        if q % 2 == 0:
                            dve(lambda e, c=c, q=q, bk=bk, tq=tq: e.tensor_scalar(out=HT[:, c, tq * 128:(tq + 1) * 128], in0=PS[bk][:, q * 128:(q + 1) * 128],
                                                                                scalar1=mcol(1, c), scalar2=mcol(0, c), op0=ALU.mult, op1=ALU.add),
                                r=["P%d" % bk, "modT"], w=["HT.%d" % tq])
                        else:
                            act(lambda e, c=c, q=q, bk=bk, tq=tq: e.activation(out=HT[:, c, tq * 128:(tq + 1) * 128], in_=PS[bk][:, q * 128:(q + 1) * 128],
                                                                             func=AF.Identity, scale=mcol(1, c), bias=mcol(0, c)),
                                r=["P%d" % bk, "modT"], w=["HT.%d" % tq])
            fence(HTW, ["HTall"])

            blocks = attn_blocks(var)
            live = [(m, lst) for m, lst in enumerate(blocks) if lst]
            ptoff = {}
            o_ = 0
            for m, lst in live:
                ptoff[m] = o_
                o_ += len(lst) * 128
            for u in range(NH):
                dma(lambda e, u=u: e.dma_start(out=TB[:], in_=tb_bf[u]), "tb", w=["TB"])
                (bq,) = acc(1)
                gemm([(bq, 0, T, lambda kc: HT[:, kc, CEN:CEN + T])], ["HTall"])
                act(lambda e, bq=bq: e.activation(out=QT[:], in_=PS[bq][:, :], func=AF.Copy), r=["P%d" % bq], w=["QT"])
                b0, b1 = acc(2)
                gemm([(b0, 0, T, lambda kc: HT[:, kc, 0:T]), (b1, 0, T, lambda kc: HT[:, kc, T:WIN])], ["HTall"])
                act(lambda e, b0=b0: e.activation(out=KT[:, 0:T], in_=PS[b0][:, :], func=AF.Copy), r=["P%d" % b0], w=["KT"])
                act(lambda e, b1=b1: e.activation(out=KT[:, T:WIN], in_=PS[b1][:, :], func=AF.Copy), r=["P%d" % b1], w=["KT"])
                b0, b1 = acc(2)
                gemm([(b0, 0, T, lambda kc: HT[:, kc, 0:T]), (b1, 0, T, lambda kc: HT[:, kc, T:WIN])], ["HTall"])
                act(lambda e, b0=b0: e.activation(out=VTS[:, 0:T], in_=PS[b0][:, :], func=AF.Copy), r=["P%d" % b0], w=["VTS"])
                act(lambda e, b1=b1: e.activation(out=VTS[:, T:WIN], in_=PS[b1][:, :], func=AF.Copy), r=["P%d" % b1], w=["VTS"])
                for idx, (m, lst) in enumerate(live):
                    p_lo = lst[0][0]
                    ncol = len(lst) * 128
                    q0 = p_lo * 128
                    sb_ = 4 if (idx % 2 == 0) else 7
                    po = ptoff[m]
                    pk = "PT.%d" % m
                    pe(lambda e, m=m, q0=q0, ncol=ncol, sb_=sb_: e.matmul(PS[sb_][:, 0:ncol], lhsT=KT[:, m * 128:(m + 1) * 128],
                                                                      rhs=QT[:, q0:q0 + ncol], start=True, stop=True),
                       r=["KT", "QT"], w=["P%d" % sb_])
                    act(lambda e, sb_=sb_, ncol=ncol, po=po: e.activation(out=PT[:, po:po + ncol], in_=PS[sb_][:, 0:ncol], func=AF.Exp, scale=SCALE),
                        r=["P%d" % sb_], w=[pk])
                    runs = []
                    for (p, ti) in lst:
                        if runs and runs[-1][1] + runs[-1][2] == ti:
                            runs[-1][2] += 1
                        else:
                            runs.append([p, ti, 1])
                    for (p, ti, n_) in runs:
                        c0 = po + (p - p_lo) * 128
                        pool(lambda e, c0=c0, ti=ti, n_=n_: e.tensor_tensor(
                            out=PT[:, c0:c0 + n_ * 128], in0=PT[:, c0:c0 + n_ * 128], in1=TB[:, ti * 128:(ti + n_) * 128], op=ALU.mult),
                            r=[pk, "TB"], w=[pk])
                vps = PS[b0][:].bitcast(BF16)
                for m in range(8):
                    pe(lambda e, m=m, vps=vps: e.transpose(out=vps[:, m * 128:(m + 1) * 128], in_=VTS[:, m * 128:(m + 1) * 128], identity=identb[:]),
                       r=["VTS", "identb"], w=["P%d" % b0])
                act(lambda e, vps=vps: e.activation(out=VV[:].rearrange("p a b -> p (a b)"), in_=vps[:, 0:1024], func=AF.Copy),
                    r=["P%d" % b0], w=["VV"])
                j = u
                ub = j % 2
                uu = U[ub]
                uk = "U%d" % ub
                ca, cb_ = (G[1], G[2]) if ub == 0 else (G[3], G[4])
                cak, cbk = ("G1", "G2") if ub == 0 else ("G3", "G4")
                a0, a1 = acc(2)
                gemm([(a0, 0, T, lambda kc: HT[:, kc, CEN - HALO:CEN - HALO + T]),
                      (a1, 0, 2 * HALO, lambda kc: HT[:, kc, CEN - HALO + T:CEN + T + HALO])], ["HTall"])
                g0, g1 = acc(2)
                gemm([(g0, 0, T, lambda kc: HT[:, kc, CEN - HALO:CEN - HALO + T]),
                      (g1, 0, 2 * HALO, lambda kc: HT[:, kc, CEN - HALO + T:CEN + T + HALO])], ["HTall"])
                act(lambda e, g0=g0: e.activation(out=SG[:, 0:T], in_=PS[g0][:, :], func=AF.Sigmoid), r=["P%d" % g0], w=["SG"])
                act(lambda e, g1=g1: e.activation(out=SG[:, T:T + 2 * HALO], in_=PS[g1][:, 0:2 * HALO], func=AF.Sigmoid), r=["P%d" % g1], w=["SG"])
                dve(lambda e, uu=uu, a0=a0: e.tensor_tensor(out=uu[:, 0:T], in0=PS[a0][:, :], in1=SG[:, 0:T], op=ALU.mult),
                    r=["P%d" % a0, "SG"], w=[uk])
                dve(lambda e, uu=uu, a1=a1: e.tensor_tensor(out=uu[:, T:T + 2 * HALO], in0=PS[a1][:, 0:2 * HALO], in1=SG[:, T:T + 2 * HALO], op=ALU.mult),
                    r=["P%d" % a1, "SG"], w=[uk])
                if var == "F":
                    dve(lambda e, uu=uu: e.memset(uu[:, 0:HALO], 0.0), r=[uk], w=[uk])
                if var == "L":
                    dve(lambda e, uu=uu: e.memset(uu[:, T + HALO:T + 2 * HALO], 0.0), r=[uk], w=[uk])
                for idx, (m, lst) in enumerate(live):
                    p_lo = lst[0][0]
                    ncol = len(lst) * 128
                    q0 = p_lo * 128
                    po = ptoff[m]
                    pk = "PT.%d" % m
                    first = idx == 0
                    last = idx == len(live) - 1
                    pe(lambda e, m=m, q0=q0, ncol=ncol, po=po, first=first, last=last: e.matmul(
                        PS[5][:, q0:q0 + ncol], lhsT=VV[:, m, :], rhs=PT[:, po:po + ncol], start=first, stop=last, skip_group_check=True),
                       r=["VV", pk], w=["P5"])
                    pe(lambda e, q0=q0, ncol=ncol, po=po, first=first, last=last: e.matmul(
                        PS[6][:, q0:q0 + ncol], lhsT=onesb[:], rhs=PT[:, po:po + ncol], start=first, stop=last, skip_group_check=True),
                       r=["onesb", pk], w=["P6"])
                dve(lambda e: e.reciprocal(out=G[0][:], in_=PS[6][:, :]), r=["P6"], w=["G0"])
                dve(lambda e, u=u: e.tensor_tensor(out=YT[:, u, :], in0=PS[5][:, :], in1=G[0][:], op=ALU.mult), r=["P5", "G0"], w=["YT.%d" % u])
                dve(lambda e, uu=uu, ca=ca, j=j: e.tensor_scalar(out=ca[:], in0=uu[:, 0:T], scalar1=wdw[:, j, 0:1], scalar2=cvp[:, 0, j:j + 1],
                                                                 op0=ALU.mult, op1=ALU.add), r=[uk, "wdw", "cvp"], w=[cak])
                dve(lambda e, uu=uu, cb_=cb_, j=j: e.tensor_scalar(out=cb_[:], in0=uu[:, 1:1 + T], scalar1=wdw[:, j, 1:2], scalar2=None, op0=ALU.mult),
                    r=[uk, "wdw"], w=[cbk])
                for k in range(2, 31):
                    cc, ck = (ca, cak) if k % 2 == 0 else (cb_, cbk)
                    dve(lambda e, uu=uu, cc=cc, j=j, k=k: e.scalar_tensor_tensor(out=cc[:], in0=uu[:, k:k + T], scalar=wdw[:, j, k:k + 1], in1=cc[:],
                                                                                 op0=ALU.mult, op1=ALU.add), r=[uk, "wdw", ck], w=[ck])
                dve(lambda e, ca=ca, cb_=cb_: e.tensor_tensor(out=ca[:], in0=ca[:], in1=cb_[:], op=ALU.add), r=[cak, cbk], w=[cak])
                act(lambda e, ca=ca, j=j: e.activation(out=YT[:, 16 + j, :], in_=ca[:], func=AF.Copy), r=[cak], w=["YT.%d" % (16 + j)])
                act(lambda e, ca=ca, cb_=cb_: e.activation(out=cb_[:], in_=ca[:], func=AF.Square), r=[cak], w=[cbk])
                if j == 0:
                    pool(lambda e, ca=ca: e.tensor_copy(out=ACC1[:], in_=ca[:]), r=[cak], w=["ACC1"])
                    pool(lambda e, cb_=cb_: e.tensor_copy(out=ACC2[:], in_=cb_[:]), r=[cbk], w=["ACC2"])
                else:
                    pool(lambda e, ca=ca: e.tensor_tensor(out=ACC1[:], in0=ACC1[:], in1=ca[:], op=ALU.add), r=[cak, "ACC1"], w=["ACC1"])
                    pool(lambda e, cb_=cb_: e.tensor_tensor(out=ACC2[:], in0=ACC2[:], in1=cb_[:], op=ALU.add), r=[cbk, "ACC2"], w=["ACC2"])
            for accT, ak, bank in ((ACC1, "ACC1", 5), (ACC2, "ACC2", 6)):
                act(lambda e, accT=accT: e.activation(out=B[0][:], in_=accT[:], func=AF.Copy), r=[ak], w=["B0"])
                dve(lambda e, accT=accT: e.tensor_tensor(out=accT[:], in0=accT[:], in1=B[0][:], op=ALU.subtract), r=[ak, "B0"], w=[ak])
                act(lambda e, accT=accT: e.activation(out=B[1][:], in_=accT[:], func=AF.Copy), r=[ak], w=["B1"])
                pe(lambda e, bank=bank: e.matmul(PS[bank][:, :], lhsT=onesb[:], rhs=B[0][:], start=True, stop=False), r=["onesb", "B0"], w=["P%d" % bank])
                pe(lambda e, bank=bank: e.matmul(PS[bank][:, :], lhsT=onesb[:], rhs=B[1][:], start=False, stop=True), r=["onesb", "B1"], w=["P%d" % bank])
            dve(lambda e: e.tensor_scalar(out=G[0][:], in0=PS[5][:, :], scalar1=1.0 / 2048, scalar2=None, op0=ALU.mult), r=["P5"], w=["G0"])
            dve(lambda e: e.tensor_tensor(out=G[2][:], in0=G[0][:], in1=G[0][:], op=ALU.mult), r=["G0"], w=["G2"])
            dve(lambda e: e.scalar_tensor_tensor(out=G[1][:], in0=PS[6][:, :], scalar=1.0 / 2048, in1=G[2][:], op0=ALU.mult, op1=ALU.subtract),
                r=["P6", "G2"], w=["G1"])
            act(lambda e: e.activation(out=G[1][:], in_=G[1][:], func=AF.Sqrt, bias=epsc[:, 1:2], scale=1.0), r=["G1", "epsc"], w=["G1"])
            dve(lambda e: e.reciprocal(out=G[1][:], in_=G[1][:]), r=["G1"], w=["G1"])
            dve(lambda e: e.scalar_tensor_tensor(out=G[2][:], in0=G[0][:], scalar=-1.0, in1=G[1][:], op0=ALU.mult, op1=ALU.mult),
                r=["G0", "G1"], w=["G2"])
            for j in range(NCV):
                tt = G[3 + j % 2]
                tk = "G%d" % (3 + j % 2)
                dve(lambda e, tt=tt, j=j: e.tensor_tensor(out=tt[:], in0=YT[:, 16 + j, :], in1=G[1][:], op=ALU.mult), r=["YT.%d" % (16 + j), "G1"], w=[tk])
                dve(lambda e, tt=tt: e.tensor_tensor(out=tt[:], in0=tt[:], in1=G[2][:], op=ALU.add), r=[tk, "G2"], w=[tk])
                act(lambda e, tt=tt, j=j: e.activation(out=YT[:, 16 + j, :], in_=tt[:], func=AF.Silu, scale=cvp[:, 1, j:j + 1], bias=cvp[:, 2, j:j + 1]),
                    r=[tk, "cvp"], w=["YT.%d" % (16 + j)])

            fence([], XTK + HTW + ["HTall"])
            for tq in range(4):
                b = tq % 2
                xs = XS[b]
                xk = "XS%d" % b
                dma(lambda e, xs=xs, tq=tq: e.dma_start(out=xs, in_=xw[slot * WIN + CEN + tq * 128: slot * WIN + CEN + (tq + 1) * 128, :]),
                    "x%d" % b, w=[xk])
                for c4 in range(8):
                    bk = 4 + (c4 % 4)
                    for q in range(4):
                        c = c4 * 4 + q
                        pe(lambda e, xs=xs, c=c, q=q, bk=bk: e.transpose(out=PS[bk][:, q * 128:(q + 1) * 128], in_=xs[:, c * 128:(c + 1) * 128], identity=ident[:]),
                           r=[xk, "ident"], w=["P%d" % bk])
                    keys = ["XT.%d" % (c4 * 4 + q) for q in range(4)]
                    if c4 % 2 == 0:
                        dve(lambda e, c4=c4, bk=bk, tq=tq: e.tensor_copy(out=XT[:, c4 * 4:c4 * 4 + 4, tq * 128:(tq + 1) * 128],
                                                                        in_=PS[bk][:, :].rearrange("p (a b) -> p a b", a=4)),
                            r=["P%d" % bk], w=keys)
                    else:
                        act(lambda e, c4=c4, bk=bk, tq=tq: e.activation(out=XT[:, c4 * 4:c4 * 4 + 4, tq * 128:(tq + 1) * 128],
                                                                       in_=PS[bk][:, :].rearrange("p (a b) -> p a b", a=4), func=AF.Copy),
                            r=["P%d" % bk], w=keys)

            for n in range(32):
                (bo,) = acc(1)
                gemm([(bo, 0, T, lambda kc: YT[:, kc, :])], YTK)
                dve(lambda e, n=n, bo=bo: e.scalar_tensor_tensor(out=XT[:, n, :], in0=PS[bo][:, :], scalar=mcol(2, n), in1=XT[:, n, :],
                                                                 op0=ALU.mult, op1=ALU.add), r=["P%d" % bo, "modT", "XT.%d" % n], w=["XT.%d" % n])

            def rms_stats():
                for c in range(KC):
                    sq = B[c % 2]
                    sk = "B%d" % (c % 2)
                    act(lambda e, sq=sq, c=c: e.activation(out=sq[:], in_=XT[:, c, :], func=AF.Square), r=["XT.%d" % c], w=[sk])
                    pe(lambda e, sq=sq, c=c: e.matmul(PS[5][:, :], lhsT=onesb[:], rhs=sq[:], start=(c == 0), stop=(c == KC - 1)),
                       r=["onesb", sk], w=["P5"])
                act(lambda e: e.activation(out=G[0][:], in_=PS[5][:, :], func=AF.Sqrt, bias=epsc[:, 0:1], scale=1.0 / D), r=["P5", "epsc"], w=["G0"])
                dve(lambda e: e.reciprocal(out=G[0][:], in_=G[0][:]), r=["G0"], w=["G0"])

            rms_stats()
            fence([], H2K + YTK + ["H2all"])
            for c in range(KC):
                tt = G[3 + c % 2]
                tk = "G%d" % (3 + c % 2)
                dve(lambda e, tt=tt, c=c: e.tensor_tensor(out=tt[:], in0=XT[:, c, :], in1=G[0][:], op=ALU.mult), r=["XT.%d" % c, "G0"], w=[tk])
                act(lambda e, tt=tt, c=c: e.activation(out=YT[:, c, :], in_=tt[:], func=AF.Identity, scale=mcol(4, c), bias=mcol(3, c)),
                    r=[tk, "modT"], w=["H2.%d" % c])
            fence(H2K, ["H2all"])

            for pi, (k0, cnt) in enumerate(PARTS):
                for fl in range(cnt):
                    (bg,) = acc(1)
                    gemm([(bg, 0, T, lambda kc: YT[:, kc, :])], ["H2all"])
                    (bu,) = acc(1)
                    gemm([(bu, 0, T, lambda kc: YT[:, kc, :])], ["H2all"])
                    sgt = G[1 + fl % 2]
                    sgk = "G%d" % (1 + fl % 2)
                    act(lambda e, sgt=sgt, bg=bg: e.activation(out=sgt[:], in_=PS[bg][:, :], func=AF.Silu), r=["P%d" % bg], w=[sgk])
                    dve(lambda e, sgt=sgt, bu=bu, fl=fl: e.tensor_tensor(out=ACTB[:, fl, :], in0=PS[bu][:, :], in1=sgt[:], op=ALU.mult),
                        r=["P%d" % bu, sgk], w=["AB.%d" % fl])
                for n in range(32):
                    (bd,) = acc(1)
                    wt, wkey = w_next()
                    for kk in range(cnt):
                        pe(lambda e, kk=kk, bd=bd, wt=wt, cnt=cnt: e.matmul(PS[bd][:, :], lhsT=wt[:, kk * 128:(kk + 1) * 128], rhs=ACTB[:, kk, :],
                                                                          start=(kk == 0), stop=(kk == cnt - 1)),
                           r=[wkey, "AB.%d" % kk], w=["P%d" % bd])
                    w_prefetch()
                    dve(lambda e, n=n, bd=bd: e.scalar_tensor_tensor(out=XT[:, n, :], in0=PS[bd][:, :], scalar=mcol(5, n), in1=XT[:, n, :],
                                                                     op0=ALU.mult, op1=ALU.add), r=["P%d" % bd, "modT", "XT.%d" % n], w=["XT.%d" % n])

            rms_stats()
            for c in range(KC):
                dve(lambda e, c=c: e.scalar_tensor_tensor(out=XT[:, c, :], in0=XT[:, c, :], scalar=gfin[:, c:c + 1], in1=G[0][:],
                                                          op0=ALU.mult, op1=ALU.mult), r=["XT.%d" % c, "gfin", "G0"], w=["XT.%d" % c])
            for tq in range(4):
                b = tq % 2
                xs = XS[b]
                xk = "XS%d" % b
                for c4 in range(8):
                    bk = 4 + (c4 % 4)
                    for q in range(4):
                        c = c4 * 4 + q
                        pe(lambda e, c=c, q=q, bk=bk, tq=tq: e.transpose(out=PS[bk][:, q * 128:(q + 1) * 128], in_=XT[:, c, tq * 128:(tq + 1) * 128], identity=ident[:]),
                           r=["XT.%d" % c, "ident"], w=["P%d" % bk])
                    if c4 % 2 == 0:
                        dve(lambda e, xs=xs, c4=c4, bk=bk: e.tensor_copy(out=xs[:, c4 * 512:(c4 + 1) * 512], in_=PS[bk][:, :]), r=["P%d" % bk], w=[xk])
                    else:
                        act(lambda e, xs=xs, c4=c4, bk=bk: e.activation(out=xs[:, c4 * 512:(c4 + 1) * 512], in_=PS[bk][:, :], func=AF.Copy), r=["P%d" % bk], w=[xk])
                dma(lambda e, xs=xs, tq=tq: e.dma_start(out=y[slot * T + tq * 128: slot * T + (tq + 1) * 128, :], in_=xs), "o%d" % b, r=[xk])

        for slot_ in range(NSLOT):
            do_slot(slot_)
        w_flush_store()
        assert wstate["next_use"] == len(wtiles), (wstate, len(wtiles))
        info = P.emit(nc, final_waits=["o0", "o1"])
    return nc, info


def slot_plan():
    plan = {}
    for i in range(4):
        plan[(i, 0)] = ("P", i, 0)
        for s in range(1, 5):
            plan[(i, s)] = ("P", i, s)
        plan[(i, 5)] = ("P", i, 7)
        c = 4 + i
        plan[(c, 0)] = ("S", i, 0)
        plan[(c, 1)] = ("S", i, 1)
        plan[(c, 2)] = ("S", i, 2)
        plan[(c, 3)] = ("P", i, 5)
        plan[(c, 4)] = ("P", i, 6)
        plan[(c, 5)] = ("S", i, 3)
    return plan


def colmajor(v, k):
    return np.ascontiguousarray(np.asarray(v, np.float32).reshape(k, 128).T)


def tile_major(w):
    K, N = w.shape
    return np.ascontiguousarray(w.reshape(K // 128, 128, N // 128, 128).transpose(2, 1, 0, 3).reshape(N // 128, 128, K))


def down_tiles(w):
    out = np.zeros((4 * 32, 128, 22 * 128), np.float32)
    for pi, (k0, cnt) in enumerate(PARTS):
        out[pi * 32:(pi + 1) * 32, :, 0:cnt * 128] = tile_major(w[k0 * 128:(k0 + cnt) * 128, :])
    return out.reshape(4 * 32 * 128, 22 * 128)


_CACHE = {}


def kernel(x_prompt, x_sample, c_prompt, c_sample, w_ada, b_ada, g_mix, w_in, rpb, w_dw, b_dw, ln_g, ln_b,
           w_o, g_ffn, w_gate, w_up, w_down, g_final):
    f = lambda a: np.asarray(a, dtype=np.float32)
    xs_ = {"P": f(x_prompt), "S": f(x_sample)}
    cs_ = {"P": f(c_prompt), "S": f(c_sample)}
    plan = slot_plan()
    if "nc" not in _CACHE:
        _CACHE["nc"] = build_nc()
    nc, info = _CACHE["nc"]

    shared = {
        "w_ada": f(w_ada)[0],
        "bada6": np.ascontiguousarray(np.repeat(colmajor(f(b_ada)[0], 192)[:, :, None], NSLOT, axis=2).reshape(128, 192 * NSLOT)),
        "g6": np.ascontiguousarray(np.repeat(np.stack([colmajor(f(g_mix)[0], KC), colmajor(f(g_ffn)[0], KC)], axis=1)[:, :, :, None],
                                             NSLOT, axis=3).reshape(128, 2 * KC * NSLOT)),
        "gfin": colmajor(f(g_final), KC),
        "w_in_t": tile_major(f(w_in)[0])[[r_ * 16 + u_ for u_ in range(16) for r_ in range(5)]].reshape(80 * 128, D),
        "w_o_t": tile_major(f(w_o)[0]).reshape(32 * 128, D),
        "w_gate_t": tile_major(f(w_gate)[0]).reshape(FC * 128, D),
        "w_up_t": tile_major(f(w_up)[0]).reshape(FC * 128, D),
        "w_down_t": down_tiles(f(w_down)[0]),
        "wdw": np.ascontiguousarray(f(w_dw)[0].T.reshape(NCV, 128, 31).transpose(1, 0, 2).reshape(128, NCV * 31)),
        "cvp": np.ascontiguousarray(np.stack([colmajor(f(b_dw)[0], NCV), colmajor(f(ln_g)[0], NCV), colmajor(f(ln_b)[0], NCV)], axis=1).reshape(128, 3 * NCV)),
        "eb": build_bias_tables(f(rpb)[0]).reshape(NH * 128, NTYPE * 128),
    }
    in_maps = []
    for core in range(8):
        xwin = np.zeros((NSLOT, WIN, D), np.float32)
        cst = np.zeros((128, KC, NSLOT), np.float32)
        for s in range(NSLOT):
            grp, si, ti = plan[(core, s)]
            xseq = xs_[grp][si]
            S = xseq.shape[0]
            t0 = ti * T - CEN
            lo, hi = max(t0, 0), min(t0 + WIN, S)
            xwin[s, lo - t0:hi - t0] = xseq[lo:hi]
            cst[:, :, s] = colmajor(cs_[grp][si], KC)
        m = dict(shared)
        m["xw"] = xwin.reshape(NSLOT * WIN, D)
        m["csT"] = np.ascontiguousarray(cst.reshape(128, KC * NSLOT))
        in_maps.append(m)
    res = run_bass_kernel_spmd(nc, in_maps, core_ids=list(range(8)))
    y_prompt = np.zeros(xs_["P"].shape, np.float32)
    y_sample = np.zeros(xs_["S"].shape, np.float32)
    outs = {"P": y_prompt, "S": y_sample}
    for core in range(8):
        yc = np.asarray(res.results[core]["y"]).reshape(NSLOT, T, D)
        for s in range(NSLOT):
            grp, si, ti = plan[(core, s)]
            outs[grp][si, ti * T:(ti + 1) * T] = yc[s]
    return (y_prompt, y_sample)
```

```python
import contextlib
import numpy as np
import concourse.bass as bass
import concourse.mybir as mybir
from concourse.bass_utils import run_bass_kernel_spmd

F32 = mybir.dt.float32
BF16 = mybir.dt.bfloat16
ALU = mybir.AluOpType
AF = mybir.ActivationFunctionType

D = 4096
KC = 32
DFF = 11008
FC = 86
NSLOT = 6
T = 512
WIN = 1024
CEN = 256
HALO = 15
NH = 16
NCV = 16
PARTS = [(0, 22), (22, 22), (44, 21), (65, 21)]
EPS = 1e-6
LN_EPS = 1e-5
NTYPE = 12
NEG = -30000.0
VARIANTS = ["F", "I", "I", "I", "I", "L"]
ENGS = ("pe", "act", "dve", "pool", "sp")


class Op:
    __slots__ = ("eng", "fn", "reads", "writes", "dma", "deps", "signal", "sem", "cnt", "idx")

    def __init__(self, eng, fn, reads, writes, dma):
        self.eng = eng
        self.fn = fn
        self.reads = reads
        self.writes = writes
        self.dma = dma
        self.deps = ()
        self.signal = dma is not None
        self.sem = None
        self.cnt = 0


class Prog:
    def __init__(self):
        self.ops = []
        self.last_w = {}
        self.readers = {}
        self.last_eng = {}
        self.last_dma = {}
        self.pending_barrier = None

    def add(self, eng, fn, reads=(), writes=(), dma=None):
        op = Op(eng, fn, tuple(reads), tuple(writes), dma)
        op.idx = len(self.ops)
        mykey = eng if dma is None else "dma:" + dma
        deps = {}
        raw = set()
        lw = self.last_w
        rd = self.readers

        def consider(p, is_raw):
            k = p.eng if p.dma is None else "dma:" + p.dma
            q = deps.get(k)
            if q is None or q.idx < p.idx:
                deps[k] = p
            if is_raw:
                raw.add(p)

        for r in op.reads:
            p = lw.get(r)
            if p is not None:
                consider(p, True)
        for w in op.writes:
            p = lw.get(w)
            if p is not None:
                consider(p, False)
            for q in rd.get(w, {}).values():
                consider(q, False)
        if self.pending_barrier is not None and eng in self.pending_barrier:
            for p in self.pending_barrier.pop(eng):
                consider(p, True)
        keep = []
        for k, p in deps.items():
            if p.eng == eng and p.dma is None and dma is None:
                if eng == "pe":
                    continue
                same_raw = [q for q in raw if q.eng == eng and q.dma is None]
                if not same_raw:
                    continue
                p = max(same_raw, key=lambda q: q.idx)
            keep.append(p)
        for p in keep:
            p.signal = True
        op.deps = tuple(keep)
        for r in op.reads:
            rd.setdefault(r, {})[mykey] = op
        for w in op.writes:
            lw[w] = op
            rd[w] = {}
        self.ops.append(op)
        if dma is None:
            self.last_eng[eng] = op
        else:
            self.last_dma[dma] = op
        return op

    def barrier(self):
        allp = list(self.last_eng.values()) + list(self.last_dma.values())
        self.pending_barrier = {e: list(allp) for e in ENGS}
        self.last_w = {}
        self.readers = {}

    def emit(self, nc, final_waits=()):
        eng_cnt = {e: 0 for e in ENGS}
        dma_cnt = {}
        for op in self.ops:
            if op.dma is not None:
                dma_cnt[op.dma] = dma_cnt.get(op.dma, 0) + 16
                op.sem = "dma:" + op.dma
                op.cnt = dma_cnt[op.dma]
            elif op.signal:
                eng_cnt[op.eng] += 1
                op.sem = "eng:" + op.eng
                op.cnt = eng_cnt[op.eng]
        sem_names = ["eng:" + e for e in ENGS if eng_cnt[e] > 0] + ["dma:" + k for k in dma_cnt]
        per_eng = {e: [o for o in self.ops if o.eng == e] for e in ENGS}
        with contextlib.ExitStack() as st:
            sems = {}
            for i, n in enumerate(sem_names):
                sems[n] = st.enter_context(nc.semaphore("s%d" % i))
            block = st.enter_context(nc.Block())

            def run(engname, eng):
                waited = {}
                for op in per_eng[engname]:
                    for p in op.deps:
                        if waited.get(p.sem, 0) < p.cnt:
                            eng.wait_ge(sems[p.sem], p.cnt)
                            waited[p.sem] = p.cnt
                    ins = op.fn(eng)
                    if op.dma is not None:
                        ins.then_inc(sems[op.sem], 16)
                    elif op.signal:
                        ins.then_inc(sems[op.sem], 1)
                if engname == "sp":
                    for k in final_waits:
                        if k in dma_cnt:
                            eng.wait_ge(sems["dma:" + k], dma_cnt[k])

            @block.tensor
            def _(e):
                run("pe", e)

            @block.scalar
            def _(e):
                run("act", e)

            @block.vector
            def _(e):
                run("dve", e)

            @block.gpsimd
            def _(e):
                run("pool", e)

            @block.sync
            def _(e):
                run("sp", e)
        return {"eng_cnt": eng_cnt, "dma_cnt": dma_cnt, "n_ops": len(self.ops)}


def attn_blocks(var):
    out = []
    for m in range(8):
        lst = []
        for p in range(4):
            special = (var == "F" and p <= 1) or (var == "L" and p >= 2)
            d = m - p
            if special:
                if 2 <= m <= 5:
                    lst.append((p, 5 + (5 - d)))
            elif 0 <= d <= 4:
                lst.append((p, 4 - d))
        out.append(lst)
    return out


def build_bias_tables(rpb):
    kpar = np.arange(2)[:, None, None, None]
    kc = np.arange(64)[None, :, None, None]
    b = np.arange(2)[None, None, :, None]
    qc = np.arange(64)[None, None, None, :]
    cs = np.clip(qc - 8, 0, 48)
    colvalid = (kc >= cs) & (kc <= cs + 15)
    coff = np.clip(kc - qc + 15, 0, 30)
    tab = np.full((NH, 2, 64, NTYPE, 2, 64), NEG, np.float32)
    for ti in range(NTYPE):
        if ti <= 4:
            d = 4 - ti
            interior = True
        else:
            d = 5 - (ti - 5)
            interior = False
        delta = 2 * d + kpar - b - 4
        ok = (delta >= -7) & (delta <= 7)
        if interior:
            ok = ok & (delta >= -4) & (delta <= 3)
        valid = np.broadcast_to(ok & colvalid, (2, 64, 2, 64))
        roff = np.broadcast_to(np.clip(delta + 7, 0, 14), (2, 64, 2, 64))
        coffb = np.broadcast_to(coff, (2, 64, 2, 64))
        vals = rpb[:, roff, coffb]
        tab[:, :, :, ti, :, :] = np.where(valid[None], vals, np.float32(NEG))
    return np.ascontiguousarray(tab.reshape(NH, 128, NTYPE * 128))


def build_nc():
    nc = bass.Bass("TRN2", target_bir_lowering=False, dynamic_dma_scratch_size=64)
    dt_in = lambda n, s: nc.dram_tensor(n, s, F32, kind="ExternalInput").ap()
    xw = dt_in("xw", [NSLOT * WIN, D])
    csT_d = dt_in("csT", [128, KC * NSLOT])
    w_ada = dt_in("w_ada", [D, 6 * D])
    bada6_d = dt_in("bada6", [128, 192 * NSLOT])
    g6_d = dt_in("g6", [128, 2 * KC * NSLOT])
    gfin_d = dt_in("gfin", [128, KC])
    w_in = dt_in("w_in_t", [80 * 128, D]).rearrange("(t p) f -> t p f", p=128)
    w_o = dt_in("w_o_t", [32 * 128, D]).rearrange("(t p) f -> t p f", p=128)
    w_gate = dt_in("w_gate_t", [FC * 128, D]).rearrange("(t p) f -> t p f", p=128)
    w_up = dt_in("w_up_t", [FC * 128, D]).rearrange("(t p) f -> t p f", p=128)
    w_down = dt_in("w_down_t", [4 * 32 * 128, 22 * 128]).rearrange("(t p) f -> t p f", p=128)
    wdw_d = dt_in("wdw", [128, NCV * 31])
    cvp_d = dt_in("cvp", [128, 3 * NCV])
    eb_d = dt_in("eb", [NH * 128, NTYPE * 128])
    y = nc.dram_tensor("y", [NSLOT * T, D], F32, kind="ExternalOutput").ap()
    win_bf = nc.dram_tensor("win_bf", [80, 128, D], BF16, kind="Internal").ap()
    wo_bf = nc.dram_tensor("wo_bf", [32, 128, D], BF16, kind="Internal").ap()
    wg_bf = nc.dram_tensor("wg_bf", [FC, 128, D], BF16, kind="Internal").ap()
    wu_bf = nc.dram_tensor("wu_bf", [FC, 128, D], BF16, kind="Internal").ap()
    wd_bf = nc.dram_tensor("wd_bf", [4 * 32, 128, 22 * 128], BF16, kind="Internal").ap()
    tb_bf = nc.dram_tensor("tb_bf", [NH, 128, NTYPE * 128], BF16, kind="Internal").ap()

    P = Prog()
    with contextlib.ExitStack() as st:
        sb = lambda name, shape, dt: st.enter_context(nc.sbuf_tensor(name, shape, dt))
        ARENA = sb("arena", [128, 44544], F32)
        R1 = ARENA[:, 0:16384]
        HT = R1.bitcast(BF16).rearrange("p (c t) -> p c t", c=KC)
        XT = R1.rearrange("p (c t) -> p c t", c=KC)
        YT = ARENA[:, 16384:24576].bitcast(BF16).rearrange("p (c t) -> p c t", c=KC)
        XS = [ARENA[:, 24576:28672], ARENA[:, 28672:32768]]
        ACTB = ARENA[:, 32768:38400].bitcast(BF16).rearrange("p (c t) -> p c t", c=22)
        JUNK = ARENA[:, 32768:34816].bitcast(BF16)
        WR = [ARENA[:, 38400 + i * 2048:38400 + (i + 1) * 2048].bitcast(BF16) for i in range(3)]
        WA = [ARENA[:, 24576 + i * 2048:24576 + (i + 1) * 2048] for i in range(3)]
        WAB = [ARENA[:, i * 1024:(i + 1) * 1024].bitcast(BF16) for i in range(3)]
        csTb = ARENA[:, 4096:4096 + 96].bitcast(BF16).rearrange("p (k s) -> p k s", s=NSLOT)
        AST = ARENA[:, 32768:36864]
        EBS = [ARENA[:, 12288 + i * 1536:12288 + (i + 1) * 1536] for i in range(2)]
        EBB = [ARENA[:, 16384 + i * 768:16384 + (i + 1) * 768].bitcast(BF16) for i in range(2)]
        csT = ARENA[:, 18432:18432 + 192].rearrange("p (k s) -> p k s", s=NSLOT)
        bada6 = ARENA[:, 18688:18688 + 1152].rearrange("p (k s) -> p k s", s=NSLOT)
        g6 = ARENA[:, 19840:19840 + 384].rearrange("p (g k s) -> p g k s", g=2, s=NSLOT)

        modT = sb("modT", [128, 192, NSLOT], F32)
        gfin = sb("gfin_sb", [128, KC], F32)
        wdw = sb("wdw_sb", [128, NCV, 31], F32)
        cvp = sb("cvp_sb", [128, 3, NCV], F32)
        ident = sb("ident", [128, 128], F32)
        identb = sb("identb", [128, 128], BF16)
        onesb = sb("onesb", [128, 128], BF16)
        epsc = sb("epsc", [128, 2], F32)
        ssq = sb("ssq", [128, 2], F32)
        rstd1 = sb("rstd1", [128, 2], F32)
        fz = sb("fz", [128, 1], F32)
        QT = sb("qT", [128, T], BF16)
        KT = sb("kT", [128, WIN], BF16)
        VTS = sb("vTs", [128, WIN], BF16)
        VV = sb("vv", [128, 8, 128], BF16)
        TB = sb("tb", [128, NTYPE * 128], BF16)
        G = [sb("g%d" % i, [128, T], F32) for i in range(5)]
        B = [sb("b%d" % i, [128, T], BF16) for i in range(2)]
        PT = sb("pt", [128, 2560], BF16)
        ACC1 = sb("acc1", [128, T], F32)
        ACC2 = sb("acc2", [128, T], F32)
        SG = sb("sg", [128, 544], F32)
        U = [sb("u%d" % i, [128, 544], F32) for i in range(2)]
        PS = [st.enter_context(nc.psum_tensor("ps%d" % i, [128, 512], F32)) for i in range(8)]

        def dve(fn, r=(), w=()):
            return P.add("dve", fn, r, w)

        def act(fn, r=(), w=()):
            return P.add("act", fn, r, w)

        def pool(fn, r=(), w=()):
            return P.add("pool", fn, r, w)

        def pe(fn, r=(), w=()):
            return P.add("pe", fn, r, w)

        def dma(fn, key, r=(), w=()):
            return P.add("sp", fn, r, w, dma=key)

        pool(lambda e: e.memset(ident[:], 0.0), w=["ident"])
        pool(lambda e: e.affine_select(out=ident[:], in_=ident[:], pattern=[[-1, 128]], compare_op=ALU.not_equal,
                                       fill=1.0, base=0, channel_multiplier=1), r=["ident"], w=["ident"])
        dve(lambda e: e.tensor_copy(out=identb[:], in_=ident[:]), r=["ident"], w=["identb"])
        dve(lambda e: e.memset(onesb[:], 1.0), w=["onesb"])
        dve(lambda e: e.memset(epsc[:, 0:1], EPS), w=["epsc"])
        dve(lambda e: e.memset(epsc[:, 1:2], LN_EPS), w=["epsc"])
        dve(lambda e: e.memset(fz[:], 0.0), w=["fz"])
        dma(lambda e: e.dma_start(out=gfin[:], in_=gfin_d[:, :]), "m_gfin", w=["gfin"])
        dma(lambda e: e.dma_start(out=wdw[:].rearrange("p a b -> p (a b)"), in_=wdw_d[:, :]), "m_wdw", w=["wdw"])
        dma(lambda e: e.dma_start(out=cvp[:].rearrange("p a b -> p (a b)"), in_=cvp_d[:, :]), "m_cvp", w=["cvp"])
        dma(lambda e: e.dma_start(out=csT.rearrange("p k s -> p (k s)"), in_=csT_d[:, :]), "m_cst", w=["csT"])
        dma(lambda e: e.dma_start(out=bada6.rearrange("p k s -> p (k s)"), in_=bada6_d[:, :]), "m_bada", w=["bada6"])
        dma(lambda e: e.dma_start(out=g6.rearrange("p g k s -> p (g k s)"), in_=g6_d[:, :]), "m_g6", w=["g6"])

        act(lambda e: e.activation(out=csTb, in_=csT, func=AF.Silu), r=["csT"], w=["csTb"])

        def ada_step(i):
            ng, kc = divmod(i, KC)
            bank = PS[ng % 2]
            bk = "P%d" % (ng % 2)
            s = i % 3
            dma(lambda e: e.dma_start(out=WA[s], in_=w_ada[kc * 128:(kc + 1) * 128, ng * 2048:(ng + 1) * 2048]),
                "wa%d" % s, w=["WA%d" % s])
            if i % 2 == 0:
                dve(lambda e: e.tensor_copy(out=WAB[s], in_=WA[s]), r=["WA%d" % s], w=["WAB%d" % s])
            else:
                act(lambda e: e.activation(out=WAB[s], in_=WA[s], func=AF.Copy), r=["WA%d" % s], w=["WAB%d" % s])
            for j in range(16):
                pe(lambda e, j=j: e.matmul(bank[:, j * 6:(j + 1) * 6], lhsT=WAB[s][:, j * 128:(j + 1) * 128],
                                           rhs=csTb[:, kc, :], start=(kc == 0 and j == 0), stop=(kc == KC - 1),
                                           skip_group_check=True),
                   r=["WAB%d" % s, "csTb"], w=[bk])
            if kc == KC - 1:
                dve(lambda e: e.tensor_tensor(out=modT[:, ng * 16:(ng + 1) * 16, :],
                                              in0=bank[:, 0:96].rearrange("p (k s) -> p k s", s=NSLOT),
                                              in1=bada6[:, ng * 16:(ng + 1) * 16, :], op=ALU.add),
                    r=[bk, "bada6"], w=["modT"])

        for i in range(12 * KC):
            ada_step(i)
        dve(lambda e: e.scalar_tensor_tensor(out=modT[:, 32:64, :], in0=modT[:, 32:64, :], scalar=1.0, in1=g6[:, 0],
                                             op0=ALU.add, op1=ALU.mult), r=["modT", "g6"], w=["modT"])
        dve(lambda e: e.scalar_tensor_tensor(out=modT[:, 128:160, :], in0=modT[:, 128:160, :], scalar=1.0, in1=g6[:, 1],
                                             op0=ALU.add, op1=ALU.mult), r=["modT", "g6"], w=["modT"])

        for h in range(NH):
            s = h % 2
            dma(lambda e, s=s, h=h: e.dma_start(out=EBS[s], in_=eb_d[h * 128:(h + 1) * 128, :]), "ebl%d" % s, w=["EBS%d" % s])
            act(lambda e, s=s: e.activation(out=EBB[s], in_=EBS[s], func=AF.Exp), r=["EBS%d" % s], w=["EBB%d" % s])
            dma(lambda e, s=s, h=h: e.dma_start(out=tb_bf[h], in_=EBB[s]), "ebs%d" % s, r=["EBB%d" % s])

        P.barrier()

        wtiles = []
        for s_ in range(NSLOT):
            lazy = (s_ == 0)
            for t in range(80):
                wtiles.append((win_bf[t], D, "S.in.%d" % t, (w_in[t], KC, "ast") if lazy else None))
            for n in range(32):
                wtiles.append((wo_bf[n], D, "S.o.%d" % n, (w_o[n], KC, "ast") if lazy else None))
            for pi, (k0, cnt) in enumerate(PARTS):
                for f in range(k0, k0 + cnt):
                    wtiles.append((wg_bf[f], D, "S.g.%d" % f, (w_gate[f], KC, "xs") if lazy else None))
                    wtiles.append((wu_bf[f], D, "S.u.%d" % f, (w_up[f], KC, "xs") if lazy else None))
                for n in range(32):
                    wtiles.append((wd_bf[pi * 32 + n][:, 0:cnt * 128], cnt * 128, "S.d.%d.%d" % (pi, n),
                                   (w_down[pi * 32 + n], cnt, "xs") if lazy else None))
        wstate = {"next_load": 0, "next_use": 0, "pending": [], "ncv": 0, "nast": 0}

        def w_flush_store(keep=0):
            while len(wstate["pending"]) > keep:
                s, dst, ncols, skey = wstate["pending"].pop(0)
                dma(lambda e, s=s, dst=dst, ncols=ncols: e.dma_start(out=dst, in_=WR[s][:, 0:ncols]), "ws%d" % s, r=["W%d" % s], w=[skey])

        def w_prefetch():
            i = wstate["next_load"]
            if i >= len(wtiles):
                w_flush_store(0)
                return
            src, ncols, skey, lazy = wtiles[i]
            w_flush_store(keep=(0 if lazy is None else 1))
            s = i % 3
            if lazy is None:
                dma(lambda e: e.dma_start(out=WR[s][:, 0:ncols], in_=src), "w%d" % s, r=[skey], w=["W%d" % s])
            else:
                fsrc, cnt, kind = lazy
                if kind == "xs":
                    c = wstate["ncv"]
                    wstate["ncv"] = c + 1
                    b = c % 2
                    stg, skeys, dkey = XS[b], ["XS%d" % b], "x%d" % b
                    eng = ("dve", "act", "pool")[c % 3]
                else:
                    c = wstate["nast"]
                    wstate["nast"] = c + 1
                    stg, skeys, dkey = AST, ["AB.%d" % k for k in range(16)], "ast"
                    eng = ("act", "act", "pool")[c % 3]
                dma(lambda e: e.dma_start(out=stg[:, 0:cnt * 128], in_=fsrc[:, 0:cnt * 128]), dkey, w=skeys)
                if eng == "act":
                    act(lambda e: e.activation(out=WR[s][:, 0:ncols], in_=stg[:, 0:ncols], func=AF.Copy), r=skeys, w=["W%d" % s])
                else:
                    P.add(eng, lambda e: e.tensor_copy(out=WR[s][:, 0:ncols], in_=stg[:, 0:ncols]), skeys, ["W%d" % s])
                wstate["pending"].append((s, src, ncols, skey))
            wstate["next_load"] = i + 1

        def w_next():
            i = wstate["next_use"]
            wstate["next_use"] = i + 1
            s = i % 3
            return WR[s], "W%d" % s

        for _ in range(3):
            w_prefetch()

        accstate = {"i": 0}

        def acc(n):
            i = accstate["i"]
            if n == 2 and i % 2 == 1:
                i += 1
            i = i % 4
            accstate["i"] = (i + n) % 4
            return list(range(i, i + n))

        def gemm(rhs_list, rkeys, nk=KC, banks=None):
            wt, wkey = w_next()
            for kc in range(nk):
                for (bk, c0, ncol, rf) in rhs_list:
                    pe(lambda e, kc=kc, bk=bk, c0=c0, ncol=ncol, rf=rf, wt=wt: e.matmul(
                        PS[bk][:, c0:c0 + ncol], lhsT=wt[:, kc * 128:(kc + 1) * 128], rhs=rf(kc),
                        start=(kc == 0), stop=(kc == nk - 1)),
                       r=[wkey] + rkeys, w=["P%d" % bk])
            w_prefetch()

        XTK = ["XT.%d" % c for c in range(KC)]
        YTK = ["YT.%d" % c for c in range(KC)]
        H2K = ["H2.%d" % c for c in range(KC)]
        HTW = ["HT.%d" % t for t in range(8)]

        def fence(r, w):
            dve(lambda e: e.memset(fz[:], 0.0), r=r, w=list(w) + ["fz"])

        SCALE = 128 ** -0.5

        def do_slot(slot):
            var = VARIANTS[slot]
            mcol = lambda kind, c: modT[:, kind * 32 + c, slot:slot + 1]
            fence([], HTW + ["HTall"] + XTK + YTK + H2K + ["H2all"])

            for tq in range(8):
                b = tq % 2
                xs = XS[b]
                xk = "XS%d" % b
                dma(lambda e, xs=xs, tq=tq: e.dma_start(out=xs, in_=xw[slot * WIN + tq * 128: slot * WIN + (tq + 1) * 128, :]),
                    "x%d" % b, w=[xk])
                jk = ["AB.%d" % k for k in range(8)]
                act(lambda e, xs=xs, b=b: e.activation(out=JUNK, in_=xs, func=AF.Square, accum_out=ssq[:, b:b + 1]),
                    r=[xk], w=jk + ["ssq%d" % b])
                act(lambda e, b=b: e.activation(out=rstd1[:, b:b + 1], in_=ssq[:, b:b + 1], func=AF.Sqrt, bias=epsc[:, 0:1], scale=1.0 / D),
                    r=["ssq%d" % b, "epsc"], w=["rstd%d" % b])
                dve(lambda e, b=b: e.reciprocal(out=rstd1[:, b:b + 1], in_=rstd1[:, b:b + 1]), r=["rstd%d" % b], w=["rstd%d" % b])
                dve(lambda e, xs=xs, b=b: e.tensor_scalar(out=xs, in0=xs, scalar1=rstd1[:, b:b + 1], scalar2=None, op0=ALU.mult),
                    r=[xk, "rstd%d" % b], w=[xk])
                for c4 in range(8):
                    bk = 4 + (c4 % 4)
                    for q in range(4):
                        c = c4 * 4 + q
                        pe(lambda e, xs=xs, c=c, q=q, bk=bk: e.transpose(out=PS[bk][:, q * 128:(q + 1) * 128], in_=xs[:, c * 128:(c + 1) * 128], identity=ident[:]),
                           r=[xk, "ident"], w=["P%d" % bk])
                    for q in range(4):
                        c = c4 * 4 + q
                        if q % 2 == 0:
                            dve(lambda e, c=c, q=q, bk=bk, tq=tq: e.tensor_scalar(out=HT[:, c, tq * 128:(tq + 1) * 128], in0=PS[bk][:, q * 128:(q + 1) * 128],
                                                                                scalar1=mcol(1, c), scalar2=mcol(0, c), op0=ALU.mult, op1=ALU.add),
                                r=["P%d" % bk, "modT"], w=["HT.%d" % tq])
                        else:
                            act(lambda e, c=c, q=q, bk=bk, tq=tq: e.activation(out=HT[:, c, tq * 128:(tq + 1) * 128], in_=PS[bk][:, q * 128:(q + 1) * 128],
                                                                             func=AF.Identity, scale=mcol(1, c), bias=mcol(0, c)),
                                r=["P%d" % bk, "modT"], w=["HT.%d" % tq])
            fence(HTW, ["HTall"])

            blocks = attn_blocks(var)
            live = [(m, lst) for m, lst in enumerate(blocks) if lst]
            ptoff = {}
            o_ = 0
            for m, lst in live:
                ptoff[m] = o_
                o_ += len(lst) * 128
            for u in range(NH):
                dma(lambda e, u=u: e.dma_start(out=TB[:], in_=tb_bf[u]), "tb", w=["TB"])
                (bq,) = acc(1)
                gemm([(bq, 0, T, lambda kc: HT[:, kc, CEN:CEN + T])], ["HTall"])
                act(lambda e, bq=bq: e.activation(out=QT[:], in_=PS[bq][:, :], func=AF.Copy), r=["P%d" % bq], w=["QT"])
                b0, b1 = acc(2)
                gemm([(b0, 0, T, lambda kc: HT[:, kc, 0:T]), (b1, 0, T, lambda kc: HT[:, kc, T:WIN])], ["HTall"])
                act(lambda e, b0=b0: e.activation(out=KT[:, 0:T], in_=PS[b0][:, :], func=AF.Copy), r=["P%d" % b0], w=["KT"])
                act(lambda e, b1=b1: e.activation(out=KT[:, T:WIN], in_=PS[b1][:, :], func=AF.Copy), r=["P%d" % b1], w=["KT"])
                b0, b1 = acc(2)
                gemm([(b0, 0, T, lambda kc: HT[:, kc, 0:T]), (b1, 0, T, lambda kc: HT[:, kc, T:WIN])], ["HTall"])
                act(lambda e, b0=b0: e.activation(out=VTS[:, 0:T], in_=PS[b0][:, :], func=AF.Copy), r=["P%d" % b0], w=["VTS"])
                act(lambda e, b1=b1: e.activation(out=VTS[:, T:WIN], in_=PS[b1][:, :], func=AF.Copy), r=["P%d" % b1], w=["VTS"])
                for idx, (m, lst) in enumerate(live):
                    p_lo = lst[0][0]
                    ncol = len(lst) * 128
                    q0 = p_lo * 128
                    sb_ = 4 if (idx % 2 == 0) else 7
                    po = ptoff[m]
                    pk = "PT.%d" % m
                    pe(lambda e, m=m, q0=q0, ncol=ncol, sb_=sb_: e.matmul(PS[sb_][:, 0:ncol], lhsT=KT[:, m * 128:(m + 1) * 128],
                                                                      rhs=QT[:, q0:q0 + ncol], start=True, stop=True),
                       r=["KT", "QT"], w=["P%d" % sb_])
                    act(lambda e, sb_=sb_, ncol=ncol, po=po: e.activation(out=PT[:, po:po + ncol], in_=PS[sb_][:, 0:ncol], func=AF.Exp, scale=SCALE),
                        r=["P%d" % sb_], w=[pk])
                    runs = []
                    for (p, ti) in lst:
                        if runs and runs[-1][1] + runs[-1][2] == ti:
                            runs[-1][2] += 1
                        else:
                            runs.append([p, ti, 1])
                    for (p, ti, n_) in runs:
                        c0 = po + (p - p_lo) * 128
                        pool(lambda e, c0=c0, ti=ti, n_=n_: e.tensor_tensor(
                            out=PT[:, c0:c0 + n_ * 128], in0=PT[:, c0:c0 + n_ * 128], in1=TB[:, ti * 128:(ti + n_) * 128], op=ALU.mult),
                            r=[pk, "TB"], w=[pk])
                vps = PS[b0][:].bitcast(BF16)
                for m in range(8):
                    pe(lambda e, m=m, vps=vps: e.transpose(out=vps[:, m * 128:(m + 1) * 128], in_=VTS[:, m * 128:(m + 1) * 128], identity=identb[:]),
                       r=["VTS", "identb"], w=["P%d" % b0])
                act(lambda e, vps=vps: e.activation(out=VV[:].rearrange("p a b -> p (a b)"), in_=vps[:, 0:1024], func=AF.Copy),
                    r=["P%d" % b0], w=["VV"])
                j = u
                ub = j % 2
                uu = U[ub]
                uk = "U%d" % ub
                ca, cb_ = (G[1], G[2]) if ub == 0 else (G[3], G[4])
                cak, cbk = ("G1", "G2") if ub == 0 else ("G3", "G4")
                a0, a1 = acc(2)
                gemm([(a0, 0, T, lambda kc: HT[:, kc, CEN - HALO:CEN - HALO + T]),
                      (a1, 0, 2 * HALO, lambda kc: HT[:, kc, CEN - HALO + T:CEN + T + HALO])], ["HTall"])
                g0, g1 = acc(2)
                gemm([(g0, 0, T, lambda kc: HT[:, kc, CEN - HALO:CEN - HALO + T]),
                      (g1, 0, 2 * HALO, lambda kc: HT[:, kc, CEN - HALO + T:CEN + T + HALO])], ["HTall"])
                act(lambda e, g0=g0: e.activation(out=SG[:, 0:T], in_=PS[g0][:, :], func=AF.Sigmoid), r=["P%d" % g0], w=["SG"])
                act(lambda e, g1=g1: e.activation(out=SG[:, T:T + 2 * HALO], in_=PS[g1][:, 0:2 * HALO], func=AF.Sigmoid), r=["P%d" % g1], w=["SG"])
                dve(lambda e, uu=uu, a0=a0: e.tensor_tensor(out=uu[:, 0:T], in0=PS[a0][:, :], in1=SG[:, 0:T], op=ALU.mult),
                    r=["P%d" % a0, "SG"], w=[uk])
                dve(lambda e, uu=uu, a1=a1: e.tensor_tensor(out=uu[:, T:T + 2 * HALO], in0=PS[a1][:, 0:2 * HALO], in1=SG[:, T:T + 2 * HALO], op=ALU.mult),
                    r=["P%d" % a1, "SG"], w=[uk])
                if var == "F":
                    dve(lambda e, uu=uu: e.memset(uu[:, 0:HALO], 0.0), r=[uk], w=[uk])
                if var == "L":
                    dve(lambda e, uu=uu: e.memset(uu[:, T + HALO:T + 2 * HALO], 0.0), r=[uk], w=[uk])
                for idx, (m, lst) in enumerate(live):
                    p_lo = lst[0][0]
                    ncol = len(lst) * 128
                    q0 = p_lo * 128
                    po = ptoff[m]
                    pk = "PT.%d" % m
                    first = idx == 0
                    last = idx == len(live) - 1
                    pe(lambda e, m=m, q0=q0, ncol=ncol, po=po, first=first, last=last: e.matmul(
                        PS[5][:, q0:q0 + ncol], lhsT=VV[:, m, :], rhs=PT[:, po:po + ncol], start=first, stop=last, skip_group_check=True),
                       r=["VV", pk], w=["P5"])
                    pe(lambda e, q0=q0, ncol=ncol, po=po, first=first, last=last: e.matmul(
                        PS[6][:, q0:q0 + ncol], lhsT=onesb[:], rhs=PT[:, po:po + ncol], start=first, stop=last, skip_group_check=True),
                       r=["onesb", pk], w=["P6"])
                dve(lambda e: e.reciprocal(out=G[0][:], in_=PS[6][:, :]), r=["P6"], w=["G0"])
                dve(lambda e, u=u: e.tensor_tensor(out=YT[:, u, :], in0=PS[5][:, :], in1=G[0][:], op=ALU.mult), r=["P5", "G0"], w=["YT.%d" % u])
                dve(lambda e, uu=uu, ca=ca, j=j: e.tensor_scalar(out=ca[:], in0=uu[:, 0:T], scalar1=wdw[:, j, 0:1], scalar2=cvp[:, 0, j:j + 1],
                                                                 op0=ALU.mult, op1=ALU.add), r=[uk, "wdw", "cvp"], w=[cak])
                dve(lambda e, uu=uu, cb_=cb_, j=j: e.tensor_scalar(out=cb_[:], in0=uu[:, 1:1 + T], scalar1=wdw[:, j, 1:2], scalar2=None, op0=ALU.mult),
                    r=[uk, "wdw"], w=[cbk])
                for k in range(2, 31):
                    cc, ck = (ca, cak) if k % 2 == 0 else (cb_, cbk)
                    dve(lambda e, uu=uu, cc=cc, j=j, k=k: e.scalar_tensor_tensor(out=cc[:], in0=uu[:, k:k + T], scalar=wdw[:, j, k:k + 1], in1=cc[:],
                                                                                 op0=ALU.mult, op1=ALU.add), r=[uk, "wdw", ck], w=[ck])
                dve(lambda e, ca=ca, cb_=cb_: e.tensor_tensor(out=ca[:], in0=ca[:], in1=cb_[:], op=ALU.add), r=[cak, cbk], w=[cak])
                act(lambda e, ca=ca, j=j: e.activation(out=YT[:, 16 + j, :], in_=ca[:], func=AF.Copy), r=[cak], w=["YT.%d" % (16 + j)])
                act(lambda e, ca=ca, cb_=cb_: e.activation(out=cb_[:], in_=ca[:], func=AF.Square), r=[cak], w=[cbk])
                if j == 0:
                    pool(lambda e, ca=ca: e.tensor_copy(out=ACC1[:], in_=ca[:]), r=[cak], w=["ACC1"])
                    pool(lambda e, cb_=cb_: e.tensor_copy(out=ACC2[:], in_=cb_[:]), r=[cbk], w=["ACC2"])
                else:
                    pool(lambda e, ca=ca: e.tensor_tensor(out=ACC1[:], in0=ACC1[:], in1=ca[:], op=ALU.add), r=[cak, "ACC1"], w=["ACC1"])
                    pool(lambda e, cb_=cb_: e.tensor_tensor(out=ACC2[:], in0=ACC2[:], in1=cb_[:], op=ALU.add), r=[cbk, "ACC2"], w=["ACC2"])
            for accT, ak, bank in ((ACC1, "ACC1", 5), (ACC2, "ACC2", 6)):
                act(lambda e, accT=accT: e.activation(out=B[0][:], in_=accT[:], func=AF.Copy), r=[ak], w=["B0"])
                dve(lambda e, accT=accT: e.tensor_tensor(out=accT[:], in0=accT[:], in1=B[0][:], op=ALU.subtract), r=[ak, "B0"], w=[ak])
                act(lambda e, accT=accT: e.activation(out=B[1][:], in_=accT[:], func=AF.Copy), r=[ak], w=["B1"])
                pe(lambda e, bank=bank: e.matmul(PS[bank][:, :], lhsT=onesb[:], rhs=B[0][:], start=True, stop=False), r=["onesb", "B0"], w=["P%d" % bank])
                pe(lambda e, bank=bank: e.matmul(PS[bank][:, :], lhsT=onesb[:], rhs=B[1][:], start=False, stop=True), r=["onesb", "B1"], w=["P%d" % bank])
            dve(lambda e: e.tensor_scalar(out=G[0][:], in0=PS[5][:, :], scalar1=1.0 / 2048, scalar2=None, op0=ALU.mult), r=["P5"], w=["G0"])
            dve(lambda e: e.tensor_tensor(out=G[2][:], in0=G[0][:], in1=G[0][:], op=ALU.mult), r=["G0"], w=["G2"])
            dve(lambda e: e.scalar_tensor_tensor(out=G[1][:], in0=PS[6][:, :], scalar=1.0 / 2048, in1=G[2][:], op0=ALU.mult, op1=ALU.subtract),
                r=["P6", "G2"], w=["G1"])
            act(lambda e: e.activation(out=G[1][:], in_=G[1][:], func=AF.Sqrt, bias=epsc[:, 1:2], scale=1.0), r=["G1", "epsc"], w=["G1"])
            dve(lambda e: e.reciprocal(out=G[1][:], in_=G[1][:]), r=["G1"], w=["G1"])
            dve(lambda e: e.scalar_tensor_tensor(out=G[2][:], in0=G[0][:], scalar=-1.0, in1=G[1][:], op0=ALU.mult, op1=ALU.mult),
                r=["G0", "G1"], w=["G2"])
            for j in range(NCV):
                tt = G[3 + j % 2]
                tk = "G%d" % (3 + j % 2)
                dve(lambda e, tt=tt, j=j: e.tensor_tensor(out=tt[:], in0=YT[:, 16 + j, :], in1=G[1][:], op=ALU.mult), r=["YT.%d" % (16 + j), "G1"], w=[tk])
                dve(lambda e, tt=tt: e.tensor_tensor(out=tt[:], in0=tt[:], in1=G[2][:], op=ALU.add), r=[tk, "G2"], w=[tk])
                act(lambda e, tt=tt, j=j: e.activation(out=YT[:, 16 + j, :], in_=tt[:], func=AF.Silu, scale=cvp[:, 1, j:j + 1], bias=cvp[:, 2, j:j + 1]),
                    r=[tk, "cvp"], w=["YT.%d" % (16 + j)])

            fence([], XTK + HTW + ["HTall"])
            for tq in range(4):
                b = tq % 2
                xs = XS[b]
                xk = "XS%d" % b
                dma(lambda e, xs=xs, tq=tq: e.dma_start(out=xs, in_=xw[slot * WIN + CEN + tq * 128: slot * WIN + CEN + (tq + 1) * 128, :]),
                    "x%d" % b, w=[xk])
                for c4 in range(8):
                    bk = 4 + (c4 % 4)
                    for q in range(4):
                        c = c4 * 4 + q
                        pe(lambda e, xs=xs, c=c, q=q, bk=bk: e.transpose(out=PS[bk][:, q * 128:(q + 1) * 128], in_=xs[:, c * 128:(c + 1) * 128], identity=ident[:]),
                           r=[xk, "ident"], w=["P%d" % bk])
                    keys = ["XT.%d" % (c4 * 4 + q) for q in range(4)]
                    if c4 % 2 == 0:
                        dve(lambda e, c4=c4, bk=bk, tq=tq: e.tensor_copy(out=XT[:, c4 * 4:c4 * 4 + 4, tq * 128:(tq + 1) * 128],
                                                                        in_=PS[bk][:, :].rearrange("p (a b) -> p a b", a=4)),
                            r=["P%d" % bk], w=keys)
                    else:
                        act(lambda e, c4=c4, bk=bk, tq=tq: e.activation(out=XT[:, c4 * 4:c4 * 4 + 4, tq * 128:(tq + 1) * 128],
                                                                       in_=PS[bk][:, :].rearrange("p (a b) -> p a b", a=4), func=AF.Copy),
                            r=["P%d" % bk], w=keys)

            for n in range(32):
                (bo,) = acc(1)
                gemm([(bo, 0, T, lambda kc: YT[:, kc, :])], YTK)
                dve(lambda e, n=n, bo=bo: e.scalar_tensor_tensor(out=XT[:, n, :], in0=PS[bo][:, :], scalar=mcol(2, n), in1=XT[:, n, :],
                                                                 op0=ALU.mult, op1=ALU.add), r=["P%d" % bo, "modT", "XT.%d" % n], w=["XT.%d" % n])

            def rms_stats():
                for c in range(KC):
                    sq = B[c % 2]
                    sk = "B%d" % (c % 2)
                    act(lambda e, sq=sq, c=c: e.activation(out=sq[:], in_=XT[:, c, :], func=AF.Square), r=["XT.%d" % c], w=[sk])
                    pe(lambda e, sq=sq, c=c: e.matmul(PS[5][:, :], lhsT=onesb[:], rhs=sq[:], start=(c == 0), stop=(c == KC - 1)),
                       r=["onesb", sk], w=["P5"])
                act(lambda e: e.activation(out=G[0][:], in_=PS[5][:, :], func=AF.Sqrt, bias=epsc[:, 0:1], scale=1.0 / D), r=["P5", "epsc"], w=["G0"])
                dve(lambda e: e.reciprocal(out=G[0][:], in_=G[0][:]), r=["G0"], w=["G0"])

            rms_stats()
            fence([], H2K + YTK + ["H2all"])
            for c in range(KC):
                tt = G[3 + c % 2]
                tk = "G%d" % (3 + c % 2)
                dve(lambda e, tt=tt, c=c: e.tensor_tensor(out=tt[:], in0=XT[:, c, :], in1=G[0][:], op=ALU.mult), r=["XT.%d" % c, "G0"], w=[tk])
                act(lambda e, tt=tt, c=c: e.activation(out=YT[:, c, :], in_=tt[:], func=AF.Identity, scale=mcol(4, c), bias=mcol(3, c)),
                    r=[tk, "modT"], w=["H2.%d" % c])
            fence(H2K, ["H2all"])

            for pi, (k0, cnt) in enumerate(PARTS):
                for fl in range(cnt):
                    (bg,) = acc(1)
                    gemm([(bg, 0, T, lambda kc: YT[:, kc, :])], ["H2all"])
                    (bu,) = acc(1)
                    gemm([(bu, 0, T, lambda kc: YT[:, kc, :])], ["H2all"])
                    sgt = G[1 + fl % 2]
                    sgk = "G%d" % (1 + fl % 2)
                    act(lambda e, sgt=sgt, bg=bg: e.activation(out=sgt[:], in_=PS[bg][:, :], func=AF.Silu), r=["P%d" % bg], w=[sgk])
                    dve(lambda e, sgt=sgt, bu=bu, fl=fl: e.tensor_tensor(out=ACTB[:, fl, :], in0=PS[bu][:, :], in1=sgt[:], op=ALU.mult),
                        r=["P%d" % bu, sgk], w=["AB.%d" % fl])
                for n in range(32):
                    (bd,) = acc(1)
                    wt, wkey = w_next()
                    for kk in range(cnt):
                        pe(lambda e, kk=kk, bd=bd, wt=wt, cnt=cnt: e.matmul(PS[bd][:, :], lhsT=wt[:, kk * 128:(kk + 1) * 128], rhs=ACTB[:, kk, :],
                                                                          start=(kk == 0), stop=(kk == cnt - 1)),
                           r=[wkey, "AB.%d" % kk], w=["P%d" % bd])
                    w_prefetch()
                    dve(lambda e, n=n, bd=bd: e.scalar_tensor_tensor(out=XT[:, n, :], in0=PS[bd][:, :], scalar=mcol(5, n), in1=XT[:, n, :],
                                                                     op0=ALU.mult, op1=ALU.add), r=["P%d" % bd, "modT", "XT.%d" % n], w=["XT.%d" % n])

            rms_stats()
            for c in range(KC):
                dve(lambda e, c=c: e.scalar_tensor_tensor(out=XT[:, c, :], in0=XT[:, c, :], scalar=gfin[:, c:c + 1], in1=G[0][:],
                                                          op0=ALU.mult, op1=ALU.mult), r=["XT.%d" % c, "gfin", "G0"], w=["XT.%d" % c])
            for tq in range(4):
                b = tq % 2
                xs = XS[b]
                xk = "XS%d" % b
                for c4 in range(8):
                    bk = 4 + (c4 % 4)
                    for q in range(4):
                        c = c4 * 4 + q
                        pe(lambda e, c=c, q=q, bk=bk, tq=tq: e.transpose(out=PS[bk][:, q * 128:(q + 1) * 128], in_=XT[:, c, tq * 128:(tq + 1) * 128], identity=ident[:]),
                           r=["XT.%d" % c, "ident"], w=["P%d" % bk])
                    if c4 % 2 == 0:
                        dve(lambda e, xs=xs, c4=c4, bk=bk: e.tensor_copy(out=xs[:, c4 * 512:(c4 + 1) * 512], in_=PS[bk][:, :]), r=["P%d" % bk], w=[xk])
                    else:
                        act(lambda e, xs=xs, c4=c4, bk=bk: e.activation(out=xs[:, c4 * 512:(c4 + 1) * 512], in_=PS[bk][:, :], func=AF.Copy), r=["P%d" % bk], w=[xk])
                dma(lambda e, xs=xs, tq=tq: e.dma_start(out=y[slot * T + tq * 128: slot * T + (tq + 1) * 128, :], in_=xs), "o%d" % b, r=[xk])

        for slot_ in range(NSLOT):
            do_slot(slot_)
        w_flush_store()
        assert wstate["next_use"] == len(wtiles), (wstate, len(wtiles))
        info = P.emit(nc, final_waits=["o0", "o1"])
    return nc, info


def slot_plan():
    plan = {}
    for i in range(4):
        plan[(i, 0)] = ("P", i, 0)
        for s in range(1, 5):
            plan[(i, s)] = ("P", i, s)
        plan[(i, 5)] = ("P", i, 7)
        c = 4 + i
        plan[(c, 0)] = ("S", i, 0)
        plan[(c, 1)] = ("S", i, 1)
        plan[(c, 2)] = ("S", i, 2)
        plan[(c, 3)] = ("P", i, 5)
        plan[(c, 4)] = ("P", i, 6)
        plan[(c, 5)] = ("S", i, 3)
    return plan


def colmajor(v, k):
    return np.ascontiguousarray(np.asarray(v, np.float32).reshape(k, 128).T)


def tile_major(w):
    K, N = w.shape
    return np.ascontiguousarray(w.reshape(K // 128, 128, N // 128, 128).transpose(2, 1, 0, 3).reshape(N // 128, 128, K))


def down_tiles(w):
    out = np.zeros((4 * 32, 128, 22 * 128), np.float32)
    for pi, (k0, cnt) in enumerate(PARTS):
        out[pi * 32:(pi + 1) * 32, :, 0:cnt * 128] = tile_major(w[k0 * 128:(k0 + cnt) * 128, :])
    return out.reshape(4 * 32 * 128, 22 * 128)


_CACHE = {}


def kernel(x_prompt, x_sample, c_prompt, c_sample, w_ada, b_ada, g_mix, w_in, rpb, w_dw, b_dw, ln_g, ln_b,
           w_o, g_ffn, w_gate, w_up, w_down, g_final):
    f = lambda a: np.asarray(a, dtype=np.float32)
    xs_ = {"P": f(x_prompt), "S": f(x_sample)}
    cs_ = {"P": f(c_prompt), "S": f(c_sample)}
    plan = slot_plan()
    if "nc" not in _CACHE:
        _CACHE["nc"] = build_nc()
    nc, info = _CACHE["nc"]

    shared = {
        "w_ada": f(w_ada)[0],
        "bada6": np.ascontiguousarray(np.repeat(colmajor(f(b_ada)[0], 192)[:, :, None], NSLOT, axis=2).reshape(128, 192 * NSLOT)),
        "g6": np.ascontiguousarray(np.repeat(np.stack([colmajor(f(g_mix)[0], KC), colmajor(f(g_ffn)[0], KC)], axis=1)[:, :, :, None],
                                             NSLOT, axis=3).reshape(128, 2 * KC * NSLOT)),
        "gfin": colmajor(f(g_final), KC),
        "w_in_t": tile_major(f(w_in)[0])[[r_ * 16 + u_ for u_ in range(16) for r_ in range(5)]].reshape(80 * 128, D),
        "w_o_t": tile_major(f(w_o)[0]).reshape(32 * 128, D),
        "w_gate_t": tile_major(f(w_gate)[0]).reshape(FC * 128, D),
        "w_up_t": tile_major(f(w_up)[0]).reshape(FC * 128, D),
        "w_down_t": down_tiles(f(w_down)[0]),
        "wdw": np.ascontiguousarray(f(w_dw)[0].T.reshape(NCV, 128, 31).transpose(1, 0, 2).reshape(128, NCV * 31)),
        "cvp": np.ascontiguousarray(np.stack([colmajor(f(b_dw)[0], NCV), colmajor(f(ln_g)[0], NCV), colmajor(f(ln_b)[0], NCV)], axis=1).reshape(128, 3 * NCV)),
        "eb": build_bias_tables(f(rpb)[0]).reshape(NH * 128, NTYPE * 128),
    }
    in_maps = []
    for core in range(8):
        xwin = np.zeros((NSLOT, WIN, D), np.float32)
        cst = np.zeros((128, KC, NSLOT), np.float32)
        for s in range(NSLOT):
            grp, si, ti = plan[(core, s)]
            xseq = xs_[grp][si]
            S = xseq.shape[0]
            t0 = ti * T - CEN
            lo, hi = max(t0, 0), min(t0 + WIN, S)
            xwin[s, lo - t0:hi - t0] = xseq[lo:hi]
            cst[:, :, s] = colmajor(cs_[grp][si], KC)
        m = dict(shared)
        m["xw"] = xwin.reshape(NSLOT * WIN, D)
        m["csT"] = np.ascontiguousarray(cst.reshape(128, KC * NSLOT))
        in_maps.append(m)
    res = run_bass_kernel_spmd(nc, in_maps, core_ids=list(range(8)))
    y_prompt = np.zeros(xs_["P"].shape, np.float32)
    y_sample = np.zeros(xs_["S"].shape, np.float32)
    outs = {"P": y_prompt, "S": y_sample}
    for core in range(8):
        yc = np.asarray(res.results[core]["y"]).reshape(NSLOT, T, D)
        for s in range(NSLOT):
            grp, si, ti = plan[(core, s)]
            outs[grp][si, ti * T:(ti + 1) * T] = yc[s]
    return (y_prompt, y_sample)
```
